# Optimizing a Trainium2 kernel written in Bass

```python
import numpy as np
import jax, jax.numpy as jnp
from jax import lax

D_MODEL = 1024
BATCH = 8
SEQ = 2048
DEPTH = 4

D_MIX = D_MODEL
D_FF = 2816
NSA_HEADS = 8
NSA_KV_GROUPS = 2
NSA_HEAD_DIM = 64
NSA_GROUP_SIZE = NSA_HEADS // NSA_KV_GROUPS
NSA_OUT = NSA_HEADS * NSA_HEAD_DIM
NSA_KV_DIM = NSA_KV_GROUPS * NSA_HEAD_DIM
CMP_BLOCK = 32
CMP_STRIDE = 16
SLC_BLOCK = 64
SLC_TOPK = 8
WINDOW = 512
Q_BLOCK = 128
POOL_WINDOWS = (2, 4, 8, 16)
POOL_GROUPS = 4
POOL_GROUP_DIM = 64
POOL_DIM = POOL_GROUPS * POOL_GROUP_DIM
GLA_HEADS = 4
GLA_KEY_DIM = 32
GLA_VAL_DIM = 64
GLA_GATE_RANK = 16
GLA_TAU = 16.0
GLA_CHUNK = 64
GLA_QK = GLA_HEADS * GLA_KEY_DIM
GLA_OUT = GLA_HEADS * GLA_VAL_DIM
IN_SPLITS = (NSA_OUT, NSA_KV_DIM, NSA_KV_DIM, NSA_KV_DIM, NSA_KV_DIM, NSA_KV_DIM, NSA_KV_DIM,
             3 * NSA_HEADS, POOL_DIM, GLA_QK, GLA_QK, GLA_OUT, GLA_GATE_RANK, GLA_OUT)
D_IN = NSA_OUT + 6 * NSA_KV_DIM + 3 * NSA_HEADS + POOL_DIM + 2 * GLA_QK + 2 * GLA_OUT + GLA_GATE_RANK
ALPHA = (2.0 * DEPTH) ** 0.25
BETA = (8.0 * DEPTH) ** -0.25
LN_EPS = 1e-5
RMS_EPS = 1e-6

kernel_name = "hymba_nsa_pool_gla_macaron_deepnorm"


def layer_norm(x, g, b):
    xf = x.astype(jnp.float32)
    mu = jnp.mean(xf, axis=-1, keepdims=True)
    var = jnp.mean(jnp.square(xf - mu), axis=-1, keepdims=True)
    y = (xf - mu) * lax.rsqrt(var + LN_EPS)
    return (y * g + b).astype(x.dtype)


def swiglu(x, wg, wu, wd):
    return (jax.nn.silu(x @ wg) * (x @ wu)) @ wd


def masked_softmax(s, mask):
    s = jnp.where(mask, s.astype(jnp.float32), -jnp.inf)
    m = jnp.max(s, axis=-1, keepdims=True)
    m = jnp.where(jnp.isfinite(m), m, 0.0)
    e = jnp.exp(s - m)
    d = jnp.sum(e, axis=-1, keepdims=True)
    return e / jnp.where(d > 0, d, 1.0)


def compress_tokens(z, pe, w1, w2):
    B, S, G, D = z.shape
    chunks = z.reshape(B, S // CMP_STRIDE, CMP_STRIDE, G, D)
    blocks = jnp.concatenate([chunks[:, :-1], chunks[:, 1:]], axis=2)
    blocks = blocks + pe[None, None, :, None, :]
    flat = jnp.transpose(blocks, (0, 1, 3, 2, 4)).reshape(B, -1, G, CMP_BLOCK * D)
    return jax.nn.gelu(flat @ w1) @ w2


def nsa_mixer(q, kc_raw, vc_raw, ks, vs, kw, vw, gates, pe_k, w1_k, w2_k, pe_v, w1_v, w2_v):
    B, S = q.shape[:2]
    G, R, HD = NSA_KV_GROUPS, NSA_GROUP_SIZE, NSA_HEAD_DIM
    q = q.reshape(B, S, G, R, HD) * (HD ** -0.5)
    kc = compress_tokens(kc_raw, pe_k, w1_k, w2_k)
    vc = compress_tokens(vc_raw, pe_v, w1_v, w2_v)
    n_cmp = kc.shape[1]
    cmp_end = jnp.arange(n_cmp) * CMP_STRIDE + CMP_BLOCK - 1
    n_slc = S // SLC_BLOCK
    topk = min(SLC_TOPK, n_slc)
    c_start = jnp.arange(n_cmp)[:, None] * CMP_STRIDE
    s_start = jnp.arange(n_slc)[None, :] * SLC_BLOCK
    overlap = ((c_start <= s_start + SLC_BLOCK - 1) & (c_start + CMP_BLOCK - 1 >= s_start)).astype(jnp.float32)
    ks_blk = jnp.transpose(ks.reshape(B, n_slc, SLC_BLOCK, G, HD), (0, 3, 1, 2, 4))
    vs_blk = jnp.transpose(vs.reshape(B, n_slc, SLC_BLOCK, G, HD), (0, 3, 1, 2, 4))
    pad = ((0, 0), (WINDOW, 0), (0, 0), (0, 0))
    kw_pad = jnp.pad(kw, pad)
    vw_pad = jnp.pad(vw, pad)
    n_qb = S // Q_BLOCK
    q_blocks = jnp.swapaxes(q.reshape(B, n_qb, Q_BLOCK, G, R, HD), 0, 1)
    g_blocks = jnp.swapaxes(gates.reshape(B, n_qb, Q_BLOCK, G, R, 3), 0, 1)
    blk_ids = jnp.arange(n_slc)
    gather = jax.vmap(jax.vmap(lambda kb, ix: kb[ix]))

    def one_block(args):
        qi, qb, gb = args
        t = qi * Q_BLOCK + jnp.arange(Q_BLOCK)
        s_c = jnp.einsum('bqgrd,bngd->bgrqn', qb, kc)
        p_c = masked_softmax(s_c, cmp_end[None, :] <= t[:, None])
        o_c = jnp.einsum('bgrqn,bngd->bqgrd', p_c.astype(vc.dtype), vc)
        imp = jnp.einsum('bgrqn,nm->bgqm', p_c, overlap)
        cur = (t // SLC_BLOCK)[:, None]
        forced = (blk_ids == 0) | (blk_ids == cur) | (blk_ids == cur - 1)
        future = blk_ids[None, :] * SLC_BLOCK > t[:, None]
        imp = jnp.where(forced, jnp.inf, jnp.where(future, -jnp.inf, imp))
        _, sel = lax.top_k(imp, topk)
        k_sel = gather(ks_blk, sel).reshape(B, G, Q_BLOCK, topk * SLC_BLOCK, HD)
        v_sel = gather(vs_blk, sel).reshape(B, G, Q_BLOCK, topk * SLC_BLOCK, HD)
        kpos = (sel[..., None] * SLC_BLOCK + jnp.arange(SLC_BLOCK)).reshape(B, G, Q_BLOCK, topk * SLC_BLOCK)
        s_s = jnp.einsum('bqgrd,bgqkd->bgrqk', qb, k_sel)
        p_s = masked_softmax(s_s, (kpos <= t[:, None])[:, :, None])
        o_s = jnp.einsum('bgrqk,bgqkd->bqgrd', p_s.astype(v_sel.dtype), v_sel)
        start = qi * Q_BLOCK
        k_win = lax.dynamic_slice_in_dim(kw_pad, start, Q_BLOCK + WINDOW, axis=1)
        v_win = lax.dynamic_slice_in_dim(vw_pad, start, Q_BLOCK + WINDOW, axis=1)
        kp = start - WINDOW + jnp.arange(Q_BLOCK + WINDOW)
        mask_w = (kp[None, :] <= t[:, None]) & (kp[None, :] > t[:, None] - WINDOW) & (kp[None, :] >= 0)
        s_w = jnp.einsum('bqgrd,bkgd->bgrqk', qb, k_win)
        p_w = masked_softmax(s_w, mask_w)
        o_w = jnp.einsum('bgrqk,bkgd->bqgrd', p_w.astype(v_win.dtype), v_win)
        o = gb[..., 0:1] * o_c + gb[..., 1:2] * o_s + gb[..., 2:3] * o_w
        return o.reshape(B, Q_BLOCK, G * R * HD)

    out = lax.map(one_block, (jnp.arange(n_qb), q_blocks, g_blocks))
    return jnp.swapaxes(out, 0, 1).reshape(B, S, NSA_OUT)


def pool_mixer(u, pool_w, pool_scale):
    B, S, _ = u.shape
    uf = u.astype(jnp.float32).reshape(B, S, POOL_GROUPS, POOL_GROUP_DIM)
    c = jnp.pad(jnp.cumsum(uf, axis=1), ((0, 0), (1, 0), (0, 0), (0, 0)))
    t = jnp.arange(S)
    means = []
    for gi, w in enumerate(POOL_WINDOWS):
        cg = c[:, :, gi]
        lower = jnp.pad(cg, ((0, 0), (w, 0), (0, 0)))[:, 1:S + 1]
        cnt = jnp.minimum(w, t + 1).astype(jnp.float32)[None, :, None]
        means.append((cg[:, 1:] - lower) / cnt)
    pooled = (jnp.stack(means, axis=2) - uf).astype(u.dtype)
    y = jnp.einsum('bsgc,gcd->bsgd', pooled, pool_w).reshape(B, S, POOL_DIM)
    return y * pool_scale


def gla_mixer(q, k, v, a_lr, r, wa2, ba, norm_g):
    B, S = q.shape[:2]
    H, DK, DV, C = GLA_HEADS, GLA_KEY_DIM, GLA_VAL_DIM, GLA_CHUNK
    nc = S // C
    log_a = jax.nn.log_sigmoid((a_lr @ wa2 + ba).astype(jnp.float32)) / GLA_TAU
    qf = q.astype(jnp.float32).reshape(B, nc, C, H, DK) * (DK ** -0.5)
    kf = k.astype(jnp.float32).reshape(B, nc, C, H, DK)
    vf = v.astype(jnp.float32).reshape(B, nc, C, H, DV)
    b = jnp.cumsum(log_a.reshape(B, nc, C, H, DK), axis=2)
    b_last = b[:, :, -1:]
    q_t = qf * jnp.exp(b)
    k_t = kf * jnp.exp(-b)
    causal = jnp.tril(jnp.ones((C, C), dtype=bool))
    A = jnp.where(causal, jnp.einsum('bnihk,bnjhk->bnhij', q_t, k_t), 0.0)
    o_intra = jnp.einsum('bnhij,bnjhv->bnihv', A, vf)
    dS = jnp.einsum('bnjhk,bnjhv->nbhkv', kf * jnp.exp(b_last - b), vf)
    decay = jnp.swapaxes(jnp.exp(b_last[:, :, 0]), 0, 1)

    def step(state, inp):
        d, ds = inp
        return d[..., None] * state + ds, state

    _, s_in = lax.scan(step, jnp.zeros((B, H, DK, DV), jnp.float32), (decay, dS))
    o_inter = jnp.einsum('bnihk,nbhkv->bnihv', q_t, s_in)
    o = (o_intra + o_inter).reshape(B, S, H, DV)
    o = o * lax.rsqrt(jnp.mean(jnp.square(o), axis=-1, keepdims=True) + RMS_EPS)
    o = o.reshape(B, S, GLA_OUT) * norm_g
    return (o * jax.nn.silu(r.astype(jnp.float32))).astype(q.dtype)


def token_mix(x, w_in, w_out, cmp_pe, cmp_w1, cmp_w2, pool_w, pool_scale, gla_wa2, gla_ba, gla_norm_g):
    B, S, _ = x.shape
    h = x @ w_in
    split_pts = np.cumsum(IN_SPLITS)[:-1].tolist()
    (nq, kc, vc, ks, vs, kw, vw, gl, u, gq, gk, gv, ga, gr) = jnp.split(h, split_pts, axis=-1)
    kv = lambda z: z.reshape(B, S, NSA_KV_GROUPS, NSA_HEAD_DIM)
    gates = jax.nn.sigmoid(gl).reshape(B, S, NSA_HEADS, 3)
    o_a = nsa_mixer(nq, kv(kc), kv(vc), kv(ks), kv(vs), kv(kw), kv(vw), gates,
                    cmp_pe[0], cmp_w1[0], cmp_w2[0], cmp_pe[1], cmp_w1[1], cmp_w2[1])
    o_b = pool_mixer(u, pool_w, pool_scale)
    o_c = gla_mixer(gq, gk, gv, ga, gr, gla_wa2, gla_ba, gla_norm_g)
    return jnp.concatenate([o_a, o_b, o_c], axis=-1) @ w_out


def setup_inputs(seed: int = 0) -> dict:
    key = jax.random.key(seed)
    ks = jax.random.split(key, 16)
    nrm = lambda k, shape, s: jax.random.normal(k, shape, jnp.float32) * s
    L = DEPTH
    return {
        "x": nrm(ks[0], (BATCH, SEQ, D_MODEL), 1.0),
        "ln_g": 1.0 + nrm(ks[1], (L, 3, D_MODEL), 0.05),
        "ln_b": nrm(ks[2], (L, 3, D_MODEL), 0.01),
        "ffn_wg": nrm(ks[3], (L, 2, D_MODEL, D_FF), D_MODEL ** -0.5),
        "ffn_wu": nrm(ks[4], (L, 2, D_MODEL, D_FF), D_MODEL ** -0.5),
        "ffn_wd": nrm(ks[5], (L, 2, D_FF, D_MODEL), BETA * D_FF ** -0.5),
        "w_in": nrm(ks[6], (L, D_MODEL, D_IN), D_MODEL ** -0.5),
        "w_out": nrm(ks[7], (L, D_MIX, D_MODEL), BETA * D_MIX ** -0.5),
        "cmp_pe": nrm(ks[8], (L, 2, CMP_BLOCK, NSA_HEAD_DIM), 0.1),
        "cmp_w1": nrm(ks[9], (L, 2, CMP_BLOCK * NSA_HEAD_DIM, NSA_HEAD_DIM), (CMP_BLOCK * NSA_HEAD_DIM) ** -0.5),
        "cmp_w2": nrm(ks[10], (L, 2, NSA_HEAD_DIM, NSA_HEAD_DIM), NSA_HEAD_DIM ** -0.5),
        "pool_w": nrm(ks[11], (L, POOL_GROUPS, POOL_GROUP_DIM, POOL_GROUP_DIM), POOL_GROUP_DIM ** -0.5),
        "pool_scale": 1.0 + nrm(ks[12], (L, POOL_DIM), 0.1),
        "gla_wa2": nrm(ks[13], (L, GLA_GATE_RANK, GLA_QK), GLA_GATE_RANK ** -0.5),
        "gla_ba": nrm(ks[14], (L, GLA_QK), 0.1),
        "gla_norm_g": 1.0 + nrm(ks[15], (L, GLA_OUT), 0.05),
    }


def reference(x, ln_g, ln_b, ffn_wg, ffn_wu, ffn_wd, w_in, w_out, cmp_pe, cmp_w1, cmp_w2,
              pool_w, pool_scale, gla_wa2, gla_ba, gla_norm_g):
    for l in range(DEPTH):
        x = layer_norm(ALPHA * x + 0.5 * swiglu(x, ffn_wg[l, 0], ffn_wu[l, 0], ffn_wd[l, 0]), ln_g[l, 0], ln_b[l, 0])
        m = token_mix(x, w_in[l], w_out[l], cmp_pe[l], cmp_w1[l], cmp_w2[l], pool_w[l], pool_scale[l],
                      gla_wa2[l], gla_ba[l], gla_norm_g[l])
        x = layer_norm(ALPHA * x + m, ln_g[l, 1], ln_b[l, 1])
        x = layer_norm(ALPHA * x + 0.5 * swiglu(x, ffn_wg[l, 1], ffn_wu[l, 1], ffn_wd[l, 1]), ln_g[l, 2], ln_b[l, 2])
    return x
```

```python
import contextlib
import os
VAR = os.environ.get('KVAR', '2')
import numpy as np
import concourse.bass as bass
import concourse.mybir as mybir
from concourse.bass_utils import run_bass_kernel_spmd

F32 = mybir.dt.float32
BF16 = mybir.dt.bfloat16
AF = mybir.ActivationFunctionType
ALU = mybir.AluOpType
AX = mybir.AxisListType

S_LEN = 2048
D = 1024
DFF = 2816
NFF = DFF // 128
NT = S_LEN // 128
DIN = 2344
DEPTH = 4
ALPHA = (2.0 * DEPTH) ** 0.25
LN_EPS = 1e-5
RMS_EPS = 1e-6
NEG = -30000.0


CONST_SHAPES = {
    "c_causal": (128, 128), "c_anti": (128, 128), "c_gmask4": (128, 4, 128), "c_cmpmask": (128, 4, 512),
    "c_onehot2": (128, 2048), "c_pb": (128, 3, 4, 128), "c_keep": (128, 16, 32), "c_add": (128, 16, 32),
    "c_hm": (128, 4), "c_scanmask": (128, 512), "c_ovl": (128, 32),
}


def consts():
    c = {"c_ident": np.eye(128, dtype=np.float32)}
    k = np.arange(128)[:, None]
    q = np.arange(128)[None, :]
    c["c_causal"] = (k <= q).astype(np.float32)
    c["c_anti"] = (k > q).astype(np.float32)
    gm = ((k // 64 == q // 64) & (k <= q)).astype(np.float32)
    c["c_gmask4"] = np.repeat(gm[:, None, :], 4, axis=1)
    n = np.arange(128)[:, None, None]
    t = (np.arange(4)[None, :, None] * 512 + np.arange(512)[None, None, :])
    c["c_cmpmask"] = ((16 * n + 31 <= t) & (n < 127)).astype(np.float32)
    oh = np.zeros((128, 2048), np.float32)
    key = np.arange(2048)
    for m in range(32):
        oh[m, key // 64 == m] = 1.0
        oh[64 + m, key // 64 == m] = 1.0
    c["c_onehot2"] = oh
    pb = np.zeros((128, 3, 4, 128), np.float32)
    tp = np.arange(128)[:, None]
    tt = np.arange(128)[None, :]
    for gi, w in enumerate((2, 4, 8, 16)):
        cur = ((tp <= tt) & (tp > tt - w)).astype(np.float32) / w
        pb[:, 0, gi, :] = cur - (tp == tt)
        prev = ((tp - 128 <= tt) & (tp - 128 > tt - w)).astype(np.float32) / w
        pb[:, 1, gi, :] = prev
        cnt = np.minimum(w, tt + 1).astype(np.float32)
        pb[:, 2, gi, :] = ((tp <= tt) & (tp > tt - w)).astype(np.float32) / cnt - (tp == tt)
    c["c_pb"] = pb
    tok = (np.arange(16)[None, :, None] * 128 + np.arange(128)[:, None, None])
    m = np.arange(32)[None, None, :]
    cur = tok // 64
    forced = (m == 0) | (m == cur) | (m == cur - 1)
    future = (m * 64 > tok)
    c["c_keep"] = (~forced & ~future).astype(np.float32)
    c["c_add"] = np.where(forced, 1e4, np.where(future, -1e4, 0.0)).astype(np.float32)
    f = np.arange(128)[:, None]
    c["c_hm"] = ((f // 32) == np.arange(4)[None, :]).astype(np.float32) * (32 ** -0.5)
    sm = np.ones((128, 512), np.float32)
    sm[:, ::64] = 0.0
    c["c_scanmask"] = sm
    nn = np.arange(128)[:, None]
    mm = np.arange(32)[None, :]
    c["c_ovl"] = ((16 * nn <= 64 * mm + 63) & (16 * nn + 31 >= 64 * mm) & (nn < 127)).astype(np.float32)
    return c


class Ctr:
    _n = 0

    def __init__(self, sem, name):
        self.sem = sem
        self.cnt = 0
        self.name = name
        self.id = Ctr._n
        Ctr._n += 1


class Eng:
    def __init__(self, eng, ctr, name):
        self.eng = eng
        self.ctr = ctr
        self.name = name
        self.seen = {}
        self.pending = 0


class Buf:
    __slots__ = ("name", "w", "r")

    def __init__(self, name):
        self.name = name
        self.w = None
        self.r = []


class Sched:
    def __init__(self, nc, stack):
        self.nc = nc
        self.stack = stack
        self.E = {}
        for nm in ("tensor", "vector", "scalar", "gpsimd", "sync"):
            sem = stack.enter_context(nc.semaphore("e_" + nm))
            self.E[nm] = Eng(getattr(nc, nm), Ctr(sem, nm), nm)
        self.dma_ctrs = []
        self.n_ins = 0
        self.outq = {}

    def dma_ctr(self, name):
        sem = self.stack.enter_context(self.nc.semaphore("d_" + name))
        c = Ctr(sem, name)
        self.dma_ctrs.append(c)
        return c

    def _wait(self, e, tok):
        ctr, val = tok
        if ctr is e.ctr and e.name == "tensor":
            return
        if e.seen.get(ctr.id, 0) >= val:
            return
        e.eng.wait_ge(ctr.sem, val)
        e.seen[ctr.id] = val
        self.n_ins += 1

    def _deps(self, e, reads, writes, skip=None):
        for b in reads:
            if b.w is not None:
                self._wait(e, b.w)
        for b in writes:
            if b.w is not None and b.w[0] is not skip:
                self._wait(e, b.w)
            for t in b.r:
                self._wait(e, t)

    def _record(self, tok, reads, writes):
        for b in reads:
            b.r.append(tok)
            if len(b.r) > 64:
                d = {}
                for c, v in b.r:
                    if c.id not in d or d[c.id][1] < v:
                        d[c.id] = (c, v)
                b.r = list(d.values())
        for b in writes:
            b.w = tok
            b.r = []

    def op(self, en, fn, reads=(), writes=(), inc=True):
        e = self.E[en]
        self._deps(e, reads, writes)
        ins = fn(e.eng)
        self.n_ins += 1
        if inc:
            e.ctr.cnt += 1
            ins.then_inc(e.ctr.sem, 1)
            tok = (e.ctr, e.ctr.cnt)
            e.pending = 0
        else:
            tok = (e.ctr, e.ctr.cnt + 1)
            e.pending += 1
        self._record(tok, reads, writes)
        return ins

    MAX_OUT = {"gpsimd": 4, "sync": 6, "scalar": 4}

    def dma(self, qn, ctr, out, in_, reads=(), writes=()):
        e = self.E[qn]
        q = self.outq.setdefault(qn, [])
        if len(q) >= self.MAX_OUT.get(qn, 4):
            c0, _ = q.pop(0)
            self._wait(e, (c0, c0.cnt))
        self._deps(e, reads, writes, skip=ctr)
        ins = e.eng.dma_start(out=out, in_=in_)
        ctr.cnt += 16
        ins.then_inc(ctr.sem, 16)
        self.n_ins += 1
        q.append((ctr, ctr.cnt))
        self._record((ctr, ctr.cnt), reads, writes)

    def barrier(self):
        for e in self.E.values():
            assert e.pending == 0, e.name
        for e in self.E.values():
            for f in self.E.values():
                if f is not e and f.ctr.cnt > 0:
                    self._wait(e, (f.ctr, f.ctr.cnt))
            for c in self.dma_ctrs:
                if c.cnt > 0:
                    self._wait(e, (c, c.cnt))


class Tile:
    def __init__(self, t, name, nbuf=1):
        self.t = t
        self.b = Buf(name)
        self.bs = [Buf(f"{name}{i}") for i in range(nbuf)]

    def __getitem__(self, k):
        return self.t[k]


def build(depth=DEPTH, dbg=None, mix=True, stage=None, stage_from=0):
    nc = bass.Bass("TRN2", target_bir_lowering=False)
    dt = nc.dram_tensor
    L = depth
    x_d = dt("x", [S_LEN, D], F32, kind="ExternalInput").ap()
    ln_g = dt("ln_g", [L, 3, D], F32, kind="ExternalInput").ap()
    ln_b = dt("ln_b", [L, 3, D], F32, kind="ExternalInput").ap()
    wg_d = dt("ffn_wg", [L, 2, D, DFF], F32, kind="ExternalInput").ap()
    wu_d = dt("ffn_wu", [L, 2, D, DFF], F32, kind="ExternalInput").ap()
    wd_d = dt("ffn_wd", [L, 2, DFF, D], F32, kind="ExternalInput").ap()
    y_d = dt("y", [S_LEN, D], F32, kind="ExternalOutput").ap()
    xd = y_d if os.environ.get("KXD", "y") == "y" else dt("xd_scratch", [S_LEN, D], F32, kind="Internal").ap()
    ident_d = dt("c_ident", [128, 128], F32, kind="ExternalInput").ap()
    w_in_d = dt("w_in", [L, D, DIN], F32, kind="ExternalInput").ap()
    w_out_d = dt("w_out", [L, D, D], F32, kind="ExternalInput").ap()
    cmp_pe_d = dt("cmp_pe", [L, 2, 32, 64], F32, kind="ExternalInput").ap()
    cmp_w1_d = dt("cmp_w1", [L, 2, 2048, 64], F32, kind="ExternalInput").ap()
    cmp_w2_d = dt("cmp_w2", [L, 2, 64, 64], F32, kind="ExternalInput").ap()
    pool_w_d = dt("pool_w", [L, 4, 64, 64], F32, kind="ExternalInput").ap()
    pool_scale_d = dt("pool_scale", [L, 256], F32, kind="ExternalInput").ap()
    wa2_d = dt("gla_wa2", [L, 16, 128], F32, kind="ExternalInput").ap()
    ba_d = dt("gla_ba", [L, 128], F32, kind="ExternalInput").ap()
    ng_d = dt("gla_norm_g", [L, 256], F32, kind="ExternalInput").ap()
    cd = {}
    for nm, shp in CONST_SHAPES.items():
        cd[nm] = dt(nm, list(shp), F32, kind="ExternalInput").ap()
    dbg_out = {}
    if dbg:
        for nm, (shp, dty) in dbg.items():
            dbg_out[nm] = dt("dbg_" + nm, list(shp), dty, kind="ExternalOutput").ap()

    stack = contextlib.ExitStack()
    with stack:
        S = Sched(nc, stack)

        def sb(name, shape, dtype, nbuf=1):
            t = stack.enter_context(nc.sbuf_tensor(name, shape, dtype))
            return Tile(t, name, nbuf)

        AT = sb("AT", [128, 8, S_LEN], BF16, nbuf=NT)
        GB = sb("GB", [128, 2, D], F32)
        ident = sb("ident", [128, 128], F32)
        xres = [sb(f"xres{i}", [128, D], F32) for i in range(2)]
        xout = [sb(f"xout{i}", [128, D], F32) for i in range(2)]
        lnst = [sb(f"lnst{i}", [128, 16], F32) for i in range(2)]
        banks = []
        for i in range(8):
            p = stack.enter_context(nc.psum_tensor(f"ps{i}", [128, 512], F32))
            banks.append(Tile(p, f"ps{i}"))
        bank_rr = [0]

        def bank():
            b = banks[bank_rr[0] % 6]
            bank_rr[0] += 1
            return b
        acc_rr = [0]

        def acc_bank():
            b = banks[6 + acc_rr[0] % 2]
            acc_rr[0] += 1
            return b

        d_xres = [S.dma_ctr(f"xres{i}") for i in range(2)]
        d_xout = [S.dma_ctr(f"xout{i}") for i in range(2)]
        d_misc = S.dma_ctr("misc")
        d_gb = S.dma_ctr("gb")
        d_w = [S.dma_ctr(f"w{i}") for i in range(2)]
        d_wd = S.dma_ctr("wd")
        XD = [Buf(f"xd{t}") for t in range(NT)]

        S.dma("sync", d_misc, ident[:], ident_d, writes=[ident.b])
        epsc = sb("epsc", [128, 2], F32)
        S.op("vector", lambda e: e.memset(epsc[:, 0:1], LN_EPS / (ALPHA * ALPHA)), writes=[epsc.b])
        S.op("vector", lambda e: e.memset(epsc[:, 1:2], RMS_EPS), writes=[epsc.b])

        ln_ctr = [0]

        def transposes_to_AT(src, t):
            for half in range(2):
                pb = bank()
                for c in range(4):
                    cc = half * 4 + c
                    S.op("tensor", lambda e: e.transpose(pb[:, c * 128:(c + 1) * 128], src[:, cc * 128:(cc + 1) * 128], ident[:]),
                         reads=[src.b, ident.b], writes=[pb.b], inc=(c == 3))
                dst = AT[:, half * 4:half * 4 + 4, t * 128:(t + 1) * 128]
                srcv = pb[:].rearrange("p (c k) -> p c k", c=4)
                if half == 0:
                    S.op("scalar", lambda e: e.copy(dst, srcv), reads=[pb.b], writes=[AT.bs[t]])
                else:
                    S.op("vector", lambda e: e.tensor_copy(dst, srcv), reads=[pb.b], writes=[AT.bs[t]])

        def load_gb(l, s):
            S.dma("sync", d_gb, GB[:, 0, :], ln_g[l, s].partition_broadcast(128), writes=[GB.b])
            S.dma("sync", d_gb, GB[:, 1, :], ln_b[l, s].partition_broadcast(128), writes=[GB.b])

        def prefetch_res(t, first):
            if nodma[0]:
                return
            i = ln_ctr[0] % 2
            src = (x_d if first else xd)[t * 128:(t + 1) * 128, :]
            S.dma("sync", d_xres[i], xres[i][:], src, reads=[XD[t]], writes=[xres[i].b])

        nodma = [False]

        def ln_tile(t, ybanks, cscale, last):
            i = ln_ctr[0] % 2
            ln_ctr[0] += 1
            xr, xo, z, st = xres[i], xout[i], xout[i], lnst[i]
            for h in range(2):
                S.op("vector", lambda e: e.scalar_tensor_tensor(out=z[:, h * 512:(h + 1) * 512], in0=ybanks[h][:], scalar=cscale,
                                                                in1=xr[:, h * 512:(h + 1) * 512], op0=ALU.mult, op1=ALU.add),
                     reads=[ybanks[h].b, xr.b], writes=[z.b])
            for h in range(2):
                S.op("vector", lambda e: e.bn_stats(st[:, h * 6:(h + 1) * 6], z[:, h * 512:(h + 1) * 512]), reads=[z.b], writes=[st.b])
            S.op("vector", lambda e: e.bn_aggr(st[:, 12:14], st[:, 0:12]), reads=[st.b], writes=[st.b])
            S.op("scalar", lambda e: e.activation(st[:, 14:15], st[:, 13:14], AF.Sqrt, bias=epsc[:, 0:1], scale=1.0),
                 reads=[st.b, epsc.b], writes=[st.b])
            S.op("vector", lambda e: e.reciprocal(st[:, 14:15], st[:, 14:15]), reads=[st.b], writes=[st.b])
            S.op("vector", lambda e: e.scalar_tensor_tensor(out=st[:, 15:16], in0=st[:, 12:13], scalar=-1.0, in1=st[:, 14:15],
                                                            op0=ALU.mult, op1=ALU.mult), reads=[st.b], writes=[st.b])
            S.op("scalar", lambda e: e.activation(z[:], z[:], AF.Identity, bias=st[:, 15:16], scale=st[:, 14:15]),
                 reads=[z.b, st.b], writes=[z.b])
            S.op("vector", lambda e: e.tensor_tensor(xo[:], z[:], GB[:, 0, :], ALU.mult), reads=[z.b, GB.b], writes=[xo.b])
            S.op("vector", lambda e: e.tensor_tensor(xo[:], xo[:], GB[:, 1, :], ALU.add), reads=[xo.b, GB.b], writes=[xo.b])
            return i

        def ln_post(t, i, last):
            xo = xout[i]
            if not last:
                transposes_to_AT(xo, t)
            dst = (y_d if last else xd)[t * 128:(t + 1) * 128, :]
            if not nodma[0]:
                S.dma("sync", d_xout[i], dst, xo[:], reads=[xo.b], writes=[XD[t]])

        for t in range(NT):
            i = t % 2
            S.dma("sync", d_xres[i], xres[i][:], x_d[t * 128:(t + 1) * 128, :], writes=[xres[i].b])
            transposes_to_AT(xres[i], t)

        def ffn(l, s, first, last):
            with contextlib.ExitStack() as fs:
                def fsb(name, shape, dtype, nbuf=1):
                    return Tile(fs.enter_context(nc.sbuf_tensor(f"{name}_{l}_{s}", shape, dtype)), name, nbuf)
                WD = fsb("WD", [128, NFF, D], BF16, nbuf=NFF)
                HT = fsb("HT", [128, NFF, 1024], BF16, nbuf=NFF * 2)
                WS = [fsb(f"WS{i}", [128, 2, 8, 256], BF16) for i in range(2)]
                SG = [fsb(f"SG{i}", [128, 512], F32) for i in range(2)]
                load_gb(l, 2 if s == 1 else 0)
                wgl, wul, wdl = wg_d[l, s], wu_d[l, s], wd_d[l, s]
                wjobs = [(H, cg) for H in range(2) for cg in range(NFF // 2)]

                def issue_w(j):
                    H, cg = wjobs[j]
                    w = WS[j % 2]
                    S.dma("gpsimd", d_w[j % 2], w[:, 0], wgl[:, cg * 256:(cg + 1) * 256].rearrange("(k p) c -> p k c", p=128), writes=[w.b])
                    S.dma("gpsimd", d_w[j % 2], w[:, 1], wul[:, cg * 256:(cg + 1) * 256].rearrange("(k p) c -> p k c", p=128), writes=[w.b])

                issue_w(0)
                issue_w(1)
                for c in range(0, NFF, 2):
                    S.dma("gpsimd", d_wd, WD[:, c:c + 2, :], wdl[c * 128:(c + 2) * 128, :].rearrange("(c p) d -> p c d", p=128),
                          writes=[WD.bs[c], WD.bs[c + 1]])
                for c in range(NFF):
                    WD.bs[c].w = (d_wd, d_wd.cnt)
                sgi = 0
                for H in range(2):
                    for cg in range(NFF // 2):
                        j = H * (NFF // 2) + cg
                        w = WS[j % 2]
                        for c2 in range(2):
                            ch = cg * 2 + c2
                            for tg in range(2):
                                tok0 = H * 1024 + tg * 512
                                atb = [AT.bs[tok0 // 128 + q] for q in range(4)]
                                pg, pu = bank(), bank()
                                for k in range(8):
                                    S.op("tensor", lambda e: e.matmul(pg[:], lhsT=w[:, 0, k, c2 * 128:(c2 + 1) * 128], rhs=AT[:, k, tok0:tok0 + 512],
                                                                      start=(k == 0), stop=(k == 7)),
                                         reads=[w.b] + atb, writes=[pg.b], inc=(k == 7))
                                for k in range(8):
                                    S.op("tensor", lambda e: e.matmul(pu[:], lhsT=w[:, 1, k, c2 * 128:(c2 + 1) * 128], rhs=AT[:, k, tok0:tok0 + 512],
                                                                      start=(k == 0), stop=(k == 7)),
                                         reads=[w.b] + atb, writes=[pu.b], inc=(k == 7))
                                sg = SG[sgi % 2]
                                sgi += 1
                                S.op("scalar", lambda e: e.activation(sg[:], pg[:], AF.Silu), reads=[pg.b], writes=[sg.b])
                                hb = HT.bs[ch * 2 + tg]
                                S.op("vector", lambda e: e.tensor_tensor(HT[:, ch, tg * 512:(tg + 1) * 512], sg[:], pu[:], ALU.mult),
                                     reads=[sg.b, pu.b], writes=[hb])
                        if j + 2 < len(wjobs):
                            issue_w(j + 2)
                    prefetch_res(H * 8, first)
                    pend = None
                    for tt in range(8):
                        t = H * 8 + tt
                        yb = [bank(), bank()]
                        for dh in range(2):
                            for c in range(NFF):
                                S.op("tensor", lambda e: e.matmul(yb[dh][:], lhsT=HT[:, c, tt * 128:(tt + 1) * 128], rhs=WD[:, c, dh * 512:(dh + 1) * 512],
                                                                  start=(c == 0), stop=(c == NFF - 1)),
                                     reads=[HT.bs[c * 2 + tt // 4], WD.bs[c]], writes=[yb[dh].b], inc=(c == NFF - 1))
                        if pend is not None:
                            ln_post(*pend)
                        pend = (t, ln_tile(t, yb, 0.5 / ALPHA, last), last)
                        if tt < 7:
                            prefetch_res(t + 1, first)
                    ln_post(*pend)
                S.barrier()

        def cload(nm, shape, dtype):
            t = sb("s_" + nm, list(shape), dtype)
            d_misc2 = S.dma_ctr("k_" + nm)
            if dtype == F32:
                S.dma("sync", d_misc2, t[:], cd[nm], writes=[t.b])
            else:
                nd = len(shape)
                if nd == 2:
                    tv, cv = t[:], cd[nm]
                else:
                    names = " ".join("abcd"[:nd - 1])
                    pat = f"p {names} -> p ({names})"
                    tv, cv = t[:].rearrange(pat), cd[nm].rearrange(pat)
                n = int(np.prod(shape[1:]))
                for c0 in range(0, n, 512):
                    c1 = min(n, c0 + 512)
                    S.dma("gpsimd", d_misc2, tv[:, c0:c1], cv[:, c0:c1], writes=[t.b])
            return t
        lw_ctrs = {}

        def d_lw_for(tile):
            if tile.b.name not in lw_ctrs:
                lw_ctrs[tile.b.name] = S.dma_ctr("lw_" + tile.b.name)
            return lw_ctrs[tile.b.name]
        causal = cload("c_causal", (128, 128), BF16)
        anti = cload("c_anti", (128, 128), BF16)
        gmask4 = cload("c_gmask4", (128, 4, 128), BF16)
        cmpmask = cload("c_cmpmask", (128, 4, 512), BF16)
        onehot2 = cload("c_onehot2", (128, 2048), BF16)
        pbc = cload("c_pb", (128, 3, 4, 128), BF16)
        keepm = cload("c_keep", (128, 16, 32), F32)
        addm = cload("c_add", (128, 16, 32), F32)
        hm = cload("c_hm", (128, 4), F32)
        scanmask = cload("c_scanmask", (128, 512), F32)
        ovl = cload("c_ovl", (128, 32), F32)
        zeros = sb("zeros", [128, 512], BF16)
        S.op("vector", lambda e: e.memset(zeros[:], 0.0), writes=[zeros.b])
        SELS = sb("SELS", [128, 4, 128], F32)
        S.op("vector", lambda e: e.memset(SELS[:], 0.0), writes=[SELS.b])

        def dump(nm, ap, rd):
            if nm in dbg_out:
                S.dma("sync", d_misc, dbg_out[nm], ap, reads=rd)

        def zero_bank(pb, m, n):
            S.op("tensor", lambda e: e.matmul(pb[0:m, 0:n], lhsT=zeros[0:1, 0:m], rhs=zeros[0:1, 0:n], start=True, stop=False),
                 reads=[zeros.b], writes=[pb.b], inc=False)

        def mixer(l, stage=stage):
            if l < stage_from:
                stage = None
            with contextlib.ExitStack() as ms:
                def msb(name, shape, dtype, nbuf=1):
                    return Tile(ms.enter_context(nc.sbuf_tensor(f"{name}_m{l}", shape, dtype)), name, nbuf)
                win = w_in_d[l]
                GQK = msb("GQK", [128, 2, S_LEN], BF16)
                GA = msb("GA", [16, S_LEN], BF16)
                UT = msb("UT", [128, NT, 256], BF16)
                GV = msb("GV", [128, NT, 256], BF16)
                NGSR = msb("NGSR", [128, NT, 256], BF16)
                WM = [msb(f"WM{i}", [128, 8, 512], BF16) for i in range(2)]
                NG = msb("NG", [128, 256], F32)
                PSC = msb("PSC", [128, 2], F32)
                BA = msb("BA", [128, 1], F32)
                WA2 = msb("WA2", [16, 128], BF16)
                PWP = msb("PWP", [64, 4, 128], BF16)
                SR = [msb(f"SR{i}", [128, 256], F32) for i in range(2)]
                nst = contextlib.ExitStack()

                def nsb0(name, shape, dtype):
                    return Tile(nst.enter_context(nc.sbuf_tensor(f"{name}_m{l}", shape, dtype)), name)
                QT2 = nsb0("QT2", [128, 4, S_LEN], BF16)
                KS2 = nsb0("KS2", [128, 2, S_LEN], BF16)
                KW2 = nsb0("KW2", [128, 2, S_LEN], BF16)
                KCV = nsb0("KCV", [128, 2, S_LEN], BF16)
                VSA = nsb0("VSA", [128, NT, 2, 65], BF16)
                VWA = nsb0("VWA", [128, NT, 2, 65], BF16)
                GATES = nsb0("GATES", [128, NT, 24], F32)
                SB2 = nsb0("SB2", [128, 2, S_LEN], BF16)
                KCMP2 = nsb0("KCMP2", [128, 2, 128], BF16)
                VCA = nsb0("VCA", [128, 2, 97], BF16)
                load_gb(l, 1)
                S.dma("sync", d_lw_for(NG), NG[:], ng_d[l].partition_broadcast(128), writes=[NG.b])
                for pr in range(2):
                    S.dma("sync", d_lw_for(PSC), PSC[:, pr:pr + 1], pool_scale_d[l, pr * 128:(pr + 1) * 128].rearrange("(p o) -> p o", o=1), writes=[PSC.b])
                S.dma("sync", d_lw_for(BA), BA[:], ba_d[l].rearrange("(p o) -> p o", o=1), writes=[BA.b])
                S.dma("gpsimd", d_lw_for(WA2), WA2[:], wa2_d[l], writes=[WA2.b])
                S.op("vector", lambda e: e.memset(PWP[:], 0.0), writes=[PWP.b])
                for g in range(4):
                    S.dma("gpsimd", d_lw_for(PWP), PWP[:, g, (g % 2) * 64:(g % 2) * 64 + 64], pool_w_d[l, g], writes=[PWP.b])
                S.op("vector", lambda e: e.memset(VSA[:, :, :, 64:65], 1.0), writes=[VSA.b])
                S.op("vector", lambda e: e.memset(VWA[:, :, :, 64:65], 1.0), writes=[VWA.b])
                S.op("vector", lambda e: e.memset(VCA[:], 0.0), writes=[VCA.b])
                S.op("vector", lambda e: e.memset(KCMP2[:], 0.0), writes=[KCMP2.b])
                S.op("vector", lambda e: e.memset(VCA[:, :, 64:65], 1.0), writes=[VCA.b])
                for g in range(2):
                    S.op("vector", lambda e: e.tensor_copy(VCA[:, g, 65:97], ovl[:]), reads=[ovl.b], writes=[VCA.b])

                def load_w(slot, specs):
                    w = WM[slot]
                    for (d0, s0, n) in specs:
                        S.dma("gpsimd", d_w[slot], w[:, :, d0:d0 + n], win[:, s0:s0 + n].rearrange("(k p) c -> p k c", p=128), writes=[w.b])
                    return w

                all_at = list(AT.bs)
                ev_rr = [0]

                def evac(dst, src, rd, wr, scale=None):
                    ev_rr[0] += 1
                    if scale is not None:
                        S.op("scalar", lambda e: e.mul(dst, src, scale), reads=rd, writes=wr)
                    elif ev_rr[0] % 2 == 0:
                        S.op("scalar", lambda e: e.copy(dst, src), reads=rd, writes=wr)
                    else:
                        S.op("vector", lambda e: e.tensor_copy(dst, src), reads=rd, writes=wr)

                def fm_job(w, chunks):
                    for tg in range(4):
                        for (c0, m, fn) in chunks:
                            pb = bank()
                            for k in range(8):
                                S.op("tensor", lambda e: e.matmul(pb[0:m, :], lhsT=w[:, k, c0:c0 + m], rhs=AT[:, k, tg * 512:(tg + 1) * 512],
                                                                  start=(k == 0), stop=(k == 7)),
                                     reads=[w.b] + all_at[tg * 4:tg * 4 + 4], writes=[pb.b], inc=(k == 7))
                            fn(pb, tg)

                def tok_job(w, ncols, fn, wc0=0):
                    for t in range(NT):
                        pb = bank()
                        for k in range(8):
                            S.op("tensor", lambda e: e.matmul(pb[:, 0:ncols], lhsT=AT[:, k, t * 128:(t + 1) * 128], rhs=w[:, k, wc0:wc0 + ncols],
                                                              start=(k == 0), stop=(k == 7)),
                                 reads=[w.b, AT.bs[t]], writes=[pb.b], inc=(k == 7))
                        fn(pb, t)

                tsl = lambda tg: slice(tg * 512, (tg + 1) * 512)
                w = load_w(0, [(0, 0, 512)])
                w2 = load_w(1, [(0, 768, 64), (64, 768, 64), (128, 832, 64), (192, 832, 64),
                                (256, 1024, 64), (320, 1024, 64), (384, 1088, 64), (448, 1088, 64)])
                fm_job(w, [(p * 128, 128, (lambda pb, tg, p=p: evac(QT2[:, p, tsl(tg)], pb[:], [pb.b], [QT2.b], scale=0.125))) for p in range(4)])
                dsts = [(KS2, 0), (KS2, 1), (KW2, 0), (KW2, 1)]
                fm_job(w2, [(i * 128, 128, (lambda pb, tg, i=i: evac(dsts[i][0][:, dsts[i][1], tsl(tg)], pb[:], [pb.b], [dsts[i][0].b]))) for i in range(4)])
                if stage == "j2":
                    S.barrier(); nst.close(); return
                w = load_w(0, [(0, 512, 256), (256, 1560, 256)])
                dst3 = [(KCV, 0), (KCV, 1), (GQK, 0), (GQK, 1)]
                fm_job(w, [(i * 128, 128, (lambda pb, tg, i=i: evac(dst3[i][0][:, dst3[i][1], tsl(tg)], pb[:], [pb.b], [dst3[i][0].b]))) for i in range(4)])
                if stage == "j3":
                    S.barrier(); nst.close(); return
                w2 = load_w(1, [(0, 896, 128), (128, 1152, 152)])

                def ev5(pb, t):
                    S.op("vector", lambda e: e.tensor_copy(VSA[:, t, :, 0:64], pb[:, 0:128].rearrange("p (g d) -> p g d", g=2)), reads=[pb.b], writes=[VSA.b])
                    S.op("vector", lambda e: e.tensor_copy(VWA[:, t, :, 0:64], pb[:, 128:256].rearrange("p (g d) -> p g d", g=2)), reads=[pb.b], writes=[VWA.b])
                    S.op("scalar", lambda e: e.activation(GATES[:, t, :], pb[:, 256:280], AF.Sigmoid), reads=[pb.b], writes=[GATES.b])
                tok_job(w2, 280, ev5)
                if stage == "j5":
                    S.barrier(); nst.close(); return
                w = load_w(1 if VAR == '1' else 0, [(0, 1304, 256), (256, 1816, 256)])

                def ev6(pb, t):
                    S.op("vector" if VAR == '2' else "scalar", (lambda e: e.tensor_copy(UT[:, t, :], pb[:, 0:256])) if VAR == '2' else (lambda e: e.copy(UT[:, t, :], pb[:, 0:256])), reads=[pb.b], writes=[UT.b])
                    S.op("vector", lambda e: e.tensor_copy(GV[:, t, :], pb[:, 256:512]), reads=[pb.b], writes=[GV.b])
                tok_job(w, 512, ev6)
                if stage == "j6":
                    S.barrier(); nst.close(); return
                w2 = load_w(1, [(0, 2072, 272)])

                def ev7(pb, t):
                    sr = SR[t % 2]
                    S.op("scalar", lambda e: e.activation(sr[:], pb[:, 0:256], AF.Silu), reads=[pb.b], writes=[sr.b])
                    S.op("vector", lambda e: e.tensor_tensor(NGSR[:, t, :], sr[:], NG[:], ALU.mult), reads=[sr.b, NG.b], writes=[NGSR.b])
                tok_job(w2, 256, ev7, wc0=16)
                fm_job(w2, [(0, 16, (lambda pb, tg: evac(GA[0:16, tsl(tg)], pb[0:16, :], [pb.b], [GA.b])))])
                WO = []
                for i in range(2):
                    wo = WM[i]
                    S.dma("gpsimd", d_w[i], wo[:], w_out_d[l][:, i * 512:(i + 1) * 512].rearrange("(k p) c -> p k c", p=128), writes=[wo.b])
                    WO.append(wo)

                if stage == "proj":
                    S.barrier(); nst.close(); return
                with contextlib.ExitStack() as cs:
                    def csb(name, shape, dtype):
                        return Tile(cs.enter_context(nc.sbuf_tensor(f"{name}_c{l}", shape, dtype)), name)
                    W1 = csb("W1", [128, 2, 32, 64], BF16)
                    PEs = csb("PEs", [32, 2, 64], F32)
                    PET = csb("PET", [64, 2, 32], BF16)
                    CB = csb("CB", [64, 2], F32)
                    W2K = csb("W2K", [64, 128], BF16)
                    W2V = csb("W2V", [64, 64], BF16)
                    GX = csb("GX", [64, 128], F32)
                    GT = csb("GT", [64, 128], F32)
                    GEL = csb("GEL", [64, 128], BF16)
                    for kv in range(2):
                        for hf in range(2):
                            w1v = cmp_w1_d[l, kv].rearrange("(j d) o -> d j o", d=64)
                            for j0 in range(0, 32, 8):
                                S.dma("gpsimd", d_lw_for(W1), W1[hf * 64:(hf + 1) * 64, kv, j0:j0 + 8, :], w1v[:, j0:j0 + 8, :], writes=[W1.b])
                        S.dma("sync", d_lw_for(PEs), PEs[:, kv, :], cmp_pe_d[l, kv], writes=[PEs.b])
                    S.dma("gpsimd", d_lw_for(W2K), W2K[:, 0:64], cmp_w2_d[l, 0], writes=[W2K.b])
                    S.dma("gpsimd", d_lw_for(W2K), W2K[:, 64:128], cmp_w2_d[l, 0], writes=[W2K.b])
                    S.dma("gpsimd", d_lw_for(W2V), W2V[:], cmp_w2_d[l, 1], writes=[W2V.b])
                    for kv in range(2):
                        pb = bank()
                        S.op("tensor", lambda e: e.transpose(pb[0:64, 0:32], PEs[:, kv, :], ident[0:32, 0:32]), reads=[PEs.b, ident.b], writes=[pb.b])
                        S.op("vector", lambda e: e.tensor_copy(PET[:, kv, :], pb[0:64, 0:32]), reads=[pb.b], writes=[PET.b])
                        pb = bank()
                        for j in range(32):
                            S.op("tensor", lambda e: e.matmul(pb[0:64, 0:1], lhsT=W1[0:64, kv, j, :], rhs=PET[:, kv, j:j + 1], start=(j == 0), stop=(j == 31)),
                                 reads=[W1.b, PET.b], writes=[pb.b], inc=(j == 31))
                        S.op("vector", lambda e: e.tensor_copy(CB[:, kv:kv + 1], pb[0:64, 0:1]), reads=[pb.b], writes=[CB.b])
                    for kv in range(2):
                        for g in range(2):
                            pb = bank()
                            for j in range(32):
                                S.op("tensor", lambda e: e.matmul(pb[0:64, 0:127], lhsT=W1[g * 64:(g + 1) * 64, kv, j, :],
                                                                  rhs=KCV[g * 64:(g + 1) * 64, kv, j:j + 16 * 126 + 1:16], start=(j == 0), stop=(j == 31)),
                                     reads=[W1.b, KCV.b], writes=[pb.b], inc=(j == 31))
                            gx, gt = GX[:, 0:127], GT[:, 0:127]
                            S.op("scalar", lambda e: e.activation(gx, pb[0:64, 0:127], AF.Identity, bias=CB[:, kv:kv + 1], scale=1.0), reads=[pb.b, CB.b], writes=[GX.b])
                            S.op("vector", lambda e: e.tensor_tensor(gt, gx, gx, ALU.mult), reads=[GX.b], writes=[GT.b])
                            S.op("vector", lambda e: e.tensor_scalar(gt, gt, 0.044715, 1.0, ALU.mult, ALU.add), reads=[GT.b], writes=[GT.b])
                            S.op("vector", lambda e: e.tensor_tensor(gt, gt, gx, ALU.mult), reads=[GT.b, GX.b], writes=[GT.b])
                            S.op("scalar", lambda e: e.activation(gt, gt, AF.Tanh, scale=0.7978845608028654), reads=[GT.b], writes=[GT.b])
                            S.op("vector", lambda e: e.tensor_scalar(gt, gt, 1.0, 0.5, ALU.add, ALU.mult), reads=[GT.b], writes=[GT.b])
                            S.op("vector", lambda e: e.tensor_tensor(GEL[:, 0:127], gt, gx, ALU.mult), reads=[GT.b, GX.b], writes=[GEL.b])
                            pb2 = bank()
                            if kv == 0:
                                S.op("tensor", lambda e: e.matmul(pb2[:, 0:127], lhsT=W2K[:], rhs=GEL[:, 0:127], start=True, stop=True), reads=[W2K.b, GEL.b], writes=[pb2.b])
                                S.op("vector", lambda e: e.tensor_copy(KCMP2[:, g, 0:127], pb2[:, 0:127]), reads=[pb2.b], writes=[KCMP2.b])
                            else:
                                S.op("tensor", lambda e: e.matmul(pb2[0:127, 0:64], lhsT=GEL[:, 0:127], rhs=W2V[:], start=True, stop=True), reads=[W2V.b, GEL.b], writes=[pb2.b])
                                S.op("vector", lambda e: e.tensor_copy(VCA[0:127, g, 0:64], pb2[0:127, 0:64]), reads=[pb2.b], writes=[VCA.b])
                    dump("kcmp", KCMP2[:], [KCMP2.b])
                    dump("vca", VCA[:], [VCA.b])
                    S.barrier()

                if stage == "cmp":
                    nst.close(); return
                with contextlib.ExitStack() as ns:
                    def nsb(name, shape, dtype):
                        return Tile(ns.enter_context(nc.sbuf_tensor(f"{name}_n{l}", shape, dtype)), name)
                    ET = [nsb(f"ET{i}", [128, 512], BF16) for i in range(3)]
                    ACC = [nsb(f"ACC{i}", [128, 4, 4, 64], F32) for i in range(2)]
                    IMP = nsb("IMP", [128, 4, 32], F32)
                    ADJ = nsb("ADJ", [128, 4, 32], F32)
                    M8 = nsb("M8", [128, 32], F32)
                    STS = [nsb(f"STS{i}", [128, 12], F32) for i in range(4)]
                    et_rr = [0]
                    st_rr = [0]

                    def stile(qap, kap, nk, c0, c1, bias, post, vap, ncol, ob, rd):
                        n = c1 - c0
                        pb = bank()
                        if bias is not None:
                            S.op("tensor", lambda e: e.matmul(pb[0:nk, 0:n], lhsT=bias[0], rhs=bias[1], start=True, stop=False),
                                 reads=[onehot2.b, SB2.b], writes=[pb.b], inc=False)
                        S.op("tensor", lambda e: e.matmul(pb[0:nk, 0:n], lhsT=kap, rhs=qap, start=(bias is None), stop=True), reads=rd, writes=[pb.b])
                        E = ET[et_rr[0] % 3]
                        et_rr[0] += 1
                        S.op("scalar", lambda e: e.activation(E[0:nk, c0:c1], pb[0:nk, 0:n], AF.Exp), reads=[pb.b], writes=[E.b])
                        for (mk, a, b_) in post:
                            S.op("vector", lambda e: e.tensor_tensor(E[0:nk, a:b_], E[0:nk, a:b_], mk, ALU.mult), reads=[E.b], writes=[E.b])
                        qts = list(range(c0 // 128, c1 // 128))

                        def pv(final=False):
                            for qt in qts:
                                S.op("tensor", lambda e: e.matmul(ob[:, qt * ncol:(qt + 1) * ncol], lhsT=E[0:nk, qt * 128:(qt + 1) * 128], rhs=vap, start=False,
                                                                  stop=(final and qt == qts[-1])),
                                     reads=[E.b] + rd, writes=[ob.b], inc=(qt == qts[-1]))
                        return pv

                    pend_fin = [None]

                    def flush_fin():
                        if pend_fin[0] is not None:
                            pend_fin[0]()
                            pend_fin[0] = None

                    def run_tiles(specs, fin):
                        pvs = None
                        for idx, sp in enumerate(specs):
                            nxt = stile(*sp)
                            if idx == 0:
                                flush_fin()
                            if pvs is not None:
                                pvs()
                            pvs = nxt
                        if pvs is not None:
                            pvs(final=True)
                        pend_fin[0] = fin

                    def finalize(ob, ncol, br, g, Q, r, acc):
                        hh = 4 * g + r
                        st = STS[st_rr[0] % 4]
                        st_rr[0] += 1
                        ob3 = ob[:, 0:4 * ncol].rearrange("p (q c) -> p q c", q=4)
                        S.op("vector", lambda e: e.tensor_scalar_max(st[:, 0:4], ob3[:, :, 64], 1e-30), reads=[ob.b], writes=[st.b])
                        S.op("vector", lambda e: e.reciprocal(st[:, 4:8], st[:, 0:4]), reads=[st.b], writes=[st.b])
                        S.op("vector", lambda e: e.tensor_tensor(st[:, 8:12], st[:, 4:8], GATES[:, 4 * Q:4 * Q + 4, hh * 3 + br], ALU.mult), reads=[st.b, GATES.b], writes=[st.b])
                        for qt in range(4):
                            if br == 0:
                                S.op("vector", lambda e: e.tensor_scalar(acc[:, qt, r, :], ob3[:, qt, 0:64], st[:, 8 + qt:9 + qt], None, ALU.mult), reads=[ob.b, st.b], writes=[acc.b])
                                if r == 0:
                                    S.op("vector", lambda e: e.tensor_scalar(IMP[:, qt, :], ob3[:, qt, 65:97], st[:, 4 + qt:5 + qt], None, ALU.mult), reads=[ob.b, st.b], writes=[IMP.b])
                                else:
                                    S.op("vector", lambda e: e.scalar_tensor_tensor(out=IMP[:, qt, :], in0=ob3[:, qt, 65:97], scalar=st[:, 4 + qt:5 + qt], in1=IMP[:, qt, :],
                                                                                    op0=ALU.mult, op1=ALU.add), reads=[ob.b, st.b, IMP.b], writes=[IMP.b])
                            else:
                                S.op("vector", lambda e: e.scalar_tensor_tensor(out=acc[:, qt, r, :], in0=ob3[:, qt, 0:64], scalar=st[:, 8 + qt:9 + qt], in1=acc[:, qt, r, :],
                                                                                op0=ALU.mult, op1=ALU.add), reads=[ob.b, st.b, acc.b], writes=[acc.b])

                    gq_i = 0
                    for g in range(2):
                        for Q in range(4):
                            acc = ACC[gq_i % 2]
                            gq_i += 1
                            qs = slice(Q * 512, (Q + 1) * 512)
                            for r in range(4):
                                hh = 4 * g + r
                                p, e_ = hh // 2, hh % 2
                                rows = slice(64 * e_, 64 * e_ + 64)
                                ob = acc_bank()
                                zero_bank(ob, 128, 4 * 97)
                                run_tiles([(QT2[rows, p, qs], KCMP2[rows, g, 0:127], 127, 0, 512, None, [(cmpmask[0:127, Q, :], 0, 512)],
                                            VCA[0:127, g, :], 97, ob, [QT2.b, KCMP2.b, VCA.b])],
                                          (lambda ob=ob, r=r: finalize(ob, 97, 0, g, Q, r, acc)))
                            flush_fin()
                            S.op("vector", lambda e: e.tensor_tensor(ADJ[:], IMP[:], keepm[:, 4 * Q:4 * Q + 4, :], ALU.mult), reads=[IMP.b, keepm.b], writes=[ADJ.b])
                            S.op("vector", lambda e: e.tensor_tensor(ADJ[:], ADJ[:], addm[:, 4 * Q:4 * Q + 4, :], ALU.add), reads=[ADJ.b, addm.b], writes=[ADJ.b])
                            pbs = bank()
                            for qt in range(4):
                                S.op("vector", lambda e: e.max(M8[:, qt * 8:(qt + 1) * 8], ADJ[:, qt, :]), reads=[ADJ.b], writes=[M8.b])
                                for off in (0, 64):
                                    S.op("vector", lambda e: e.tensor_scalar(SELS[:, qt, off:off + 32], ADJ[:, qt, :], M8[:, qt * 8 + 7:qt * 8 + 8], NEG, ALU.is_lt, ALU.mult),
                                         reads=[ADJ.b, M8.b], writes=[SELS.b])
                                S.op("tensor", lambda e: e.transpose(pbs[:, qt * 128:(qt + 1) * 128], SELS[:, qt, :], ident[:]), reads=[SELS.b, ident.b], writes=[pbs.b], inc=True)
                            S.op("vector", lambda e: e.tensor_copy(SB2[:, g, qs], pbs[:]), reads=[pbs.b], writes=[SB2.b])
                            for r in range(4):
                                hh = 4 * g + r
                                p, e_ = hh // 2, hh % 2
                                rows = slice(64 * e_, 64 * e_ + 64)
                                ob = acc_bank()
                                zero_bank(ob, 128, 4 * 65)
                                specs = []
                                for j in range(0, 4 * Q + 4):
                                    d = j - 4 * Q
                                    c0 = 128 * d if d > 0 else 0
                                    post = [(causal[:], 128 * d, 128 * d + 128)] if d >= 0 else []
                                    ks = slice(j * 128, (j + 1) * 128)
                                    specs.append((QT2[rows, p, Q * 512 + c0:(Q + 1) * 512], KS2[rows, g, ks], 128, c0, 512,
                                                  (onehot2[rows, ks], SB2[rows, g, Q * 512 + c0:(Q + 1) * 512]), post, VSA[:, j, g, :], 65, ob, [QT2.b, KS2.b, VSA.b]))
                                run_tiles(specs, (lambda ob=ob, r=r: finalize(ob, 65, 1, g, Q, r, acc)))
                                ob = acc_bank()
                                zero_bank(ob, 128, 4 * 65)
                                specs = []
                                for j in range(max(0, 4 * Q - 4), 4 * Q + 4):
                                    d = j - 4 * Q
                                    ks = slice(j * 128, (j + 1) * 128)
                                    if d >= 0:
                                        c0, c1 = 128 * d, 512
                                        post = [(causal[:], 128 * d, 128 * d + 128)]
                                    else:
                                        dd = d + 4
                                        c0, c1 = 0, 128 * (dd + 1)
                                        post = [(anti[:], 128 * dd, 128 * dd + 128)]
                                    specs.append((QT2[rows, p, Q * 512 + c0:Q * 512 + c1], KW2[rows, g, ks], 128, c0, c1, None, post, VWA[:, j, g, :], 65, ob, [QT2.b, KW2.b, VWA.b]))
                                run_tiles(specs, (lambda ob=ob, r=r: finalize(ob, 65, 2, g, Q, r, acc)))
                            flush_fin()
                            for b2 in range(2):
                                pbt = bank()
                                for qt in range(4):
                                    S.op("tensor", lambda e: e.transpose(pbt[:, qt * 128:(qt + 1) * 128], acc[:, qt, 2 * b2:2 * b2 + 2, :].rearrange("p r d -> p (r d)"), ident[:]),
                                         reads=[acc.b, ident.b], writes=[pbt.b], inc=(qt == 3))
                                evac(AT[:, 2 * g + b2, qs], pbt[:], [pbt.b], all_at[4 * Q:4 * Q + 4])
                    S.barrier()
                nst.close()

                with contextlib.ExitStack() as ps_:
                    PT = [Tile(ps_.enter_context(nc.sbuf_tensor(f"PT{i}_p{l}", [64, 4, 128], BF16)), f"PT{i}") for i in range(2)]
                    for t in range(NT):
                        pb = bank()
                        for g in range(4):
                            kind = 2 if t == 0 else 0
                            S.op("tensor", lambda e: e.matmul(pb[0:64, g * 128:(g + 1) * 128], lhsT=UT[:, t, g * 64:(g + 1) * 64], rhs=pbc[:, kind, g, :], start=True, stop=(t == 0)),
                                 reads=[UT.b, pbc.b], writes=[pb.b], inc=(t == 0 and g == 3))
                            if t > 0:
                                S.op("tensor", lambda e: e.matmul(pb[0:64, g * 128:(g + 1) * 128], lhsT=UT[:, t - 1, g * 64:(g + 1) * 64], rhs=pbc[:, 1, g, :], start=False, stop=True),
                                     reads=[UT.b, pbc.b], writes=[pb.b], inc=(g == 3))
                        pt = PT[t % 2]
                        S.op("vector", lambda e: e.tensor_copy(pt[:], pb[0:64, :].rearrange("p (g k) -> p g k", g=4)), reads=[pb.b], writes=[pt.b])
                        pb2 = bank()
                        for pr in range(2):
                            for gg in range(2):
                                g = pr * 2 + gg
                                S.op("tensor", lambda e: e.matmul(pb2[:, pr * 128:(pr + 1) * 128], lhsT=PWP[:, g, :], rhs=pt[:, g, :], start=(gg == 0), stop=(gg == 1)),
                                     reads=[PWP.b, pt.b], writes=[pb2.b], inc=(gg == 1))
                            S.op("scalar", lambda e: e.activation(AT[:, 4 + pr, t * 128:(t + 1) * 128], pb2[:, pr * 128:(pr + 1) * 128], AF.Copy, scale=PSC[:, pr:pr + 1]),
                                 reads=[pb2.b, PSC.b], writes=[AT.bs[t]])

                if stage == "pool":
                    S.barrier(); return
                with contextlib.ExitStack() as gs:
                    def gsb(name, shape, dtype):
                        return Tile(gs.enter_context(nc.sbuf_tensor(f"{name}_g{l}", shape, dtype)), name)
                    LA = gsb("LA", [128, 512], F32)
                    B16 = gsb("B16", [128, 512], F32)
                    EB = gsb("EB", [128, 512], F32)
                    ENB = gsb("ENB", [128, 512], F32)
                    QM = gsb("QM", [128, 4, 4, 128], BF16)
                    KT = gsb("KT", [128, 512], BF16)
                    KD = gsb("KD", [128, 512], F32)
                    KDT = gsb("KDT", [128, 4, 128], BF16)
                    SST = [gsb(f"SST{i}", [128, 256], F32) for i in range(2)]
                    SBF = gsb("SBF", [128, 9, 256], BF16)
                    AM = [gsb(f"AM{i}", [128, 4, 128], BF16) for i in range(2)]
                    OSB = [gsb(f"OSB{i}", [128, 256], F32) for i in range(2)]
                    SQ = gsb("SQ", [128, 256], F32)
                    OG = [gsb(f"OG{i}", [128, 256], F32) for i in range(2)]
                    RS = [gsb(f"RS{i}", [128, 8], F32) for i in range(2)]
                    S.op("vector", lambda e: e.memset(SST[0][:], 0.0), writes=[SST[0].b])
                    S.op("vector", lambda e: e.memset(SBF[:, 0, :], 0.0), writes=[SBF.b])
                    si = 0
                    for tg in range(4):
                        ts_ = slice(tg * 512, (tg + 1) * 512)
                        pb = bank()
                        S.op("tensor", lambda e: e.matmul(pb[:], lhsT=WA2[:], rhs=GA[0:16, ts_], start=True, stop=True), reads=[WA2.b, GA.b], writes=[pb.b])
                        S.op("scalar", lambda e: e.activation(LA[:], pb[:], AF.Sigmoid, bias=BA[:, 0:1], scale=1.0), reads=[pb.b, BA.b], writes=[LA.b])
                        S.op("scalar", lambda e: e.activation(LA[:], LA[:], AF.Ln), reads=[LA.b], writes=[LA.b])
                        S.op("vector", lambda e: e.tensor_tensor_scan(B16[:], scanmask[:], LA[:], 0.0, ALU.mult, ALU.add), reads=[LA.b, scanmask.b], writes=[B16.b])
                        S.op("scalar", lambda e: e.activation(EB[:], B16[:], AF.Exp, scale=1.0 / 16.0), reads=[B16.b], writes=[EB.b])
                        S.op("scalar", lambda e: e.activation(ENB[:], B16[:], AF.Exp, scale=-1.0 / 16.0), reads=[B16.b], writes=[ENB.b])
                        for h in range(4):
                            S.op("vector", lambda e: e.scalar_tensor_tensor(out=QM[:, :, h, :], in0=GQK[:, 0, ts_].rearrange("p (t k) -> p t k", t=4), scalar=hm[:, h:h + 1],
                                                                            in1=EB[:].rearrange("p (t k) -> p t k", t=4), op0=ALU.mult, op1=ALU.mult),
                                 reads=[GQK.b, hm.b, EB.b], writes=[QM.b])
                        S.op("vector", lambda e: e.tensor_tensor(KT[:], GQK[:, 1, ts_], ENB[:], ALU.mult), reads=[GQK.b, ENB.b], writes=[KT.b])
                        for c in range(8):
                            S.op("vector", lambda e: e.tensor_scalar(KD[:, c * 64:(c + 1) * 64], KT[:, c * 64:(c + 1) * 64], EB[:, c * 64 + 63:c * 64 + 64], None, ALU.mult),
                                 reads=[KT.b, EB.b], writes=[KD.b])
                        pb = bank()
                        for tt in range(4):
                            S.op("tensor", lambda e: e.transpose(pb[:, tt * 128:(tt + 1) * 128], KD[:, tt * 128:(tt + 1) * 128], ident[:]), reads=[KD.b, ident.b], writes=[pb.b], inc=(tt == 3))
                        S.op("vector", lambda e: e.tensor_copy(KDT[:], pb[:].rearrange("p (t k) -> p t k", t=4)), reads=[pb.b], writes=[KDT.b])
                        for c in range(8):
                            tt, hf = c // 2, c % 2
                            rows = slice(64 * hf, 64 * hf + 64)
                            pbs = bank()
                            S.op("tensor", lambda e: e.matmul(pbs[:, 0:256], lhsT=KDT[rows, tt, :], rhs=GV[rows, 4 * tg + tt, :], start=True, stop=True),
                                 reads=[KDT.b, GV.b], writes=[pbs.b])
                            so, sn = SST[si % 2], SST[(si + 1) % 2]
                            si += 1
                            S.op("vector", lambda e: e.scalar_tensor_tensor(out=sn[:], in0=so[:], scalar=EB[:, c * 64 + 63:c * 64 + 64], in1=pbs[:, 0:256], op0=ALU.mult, op1=ALU.add),
                                 reads=[so.b, EB.b, pbs.b], writes=[sn.b])
                            S.op("scalar", lambda e: e.activation(SBF[:, c + 1, :], sn[:], AF.Identity), reads=[sn.b], writes=[SBF.b])
                        for tt in range(4):
                            t = 4 * tg + tt
                            pa = bank()
                            S.op("tensor", lambda e: e.matmul(pa[:], lhsT=KT[:, tt * 128:(tt + 1) * 128], rhs=QM[:, tt, :, :].rearrange("p h k -> p (h k)"), start=True, stop=True),
                                 reads=[KT.b, QM.b], writes=[pa.b])
                            am = AM[tt % 2]
                            S.op("vector", lambda e: e.tensor_tensor(am[:], pa[:].rearrange("p (h k) -> p h k", h=4), gmask4[:], ALU.mult), reads=[pa.b, gmask4.b], writes=[am.b])
                            po = bank()
                            zero_bank(po, 128, 256)
                            for h in range(4):
                                for hf in range(2):
                                    S.op("tensor", lambda e: e.matmul(po[64 * hf:64 * hf + 64, 64 * h:64 * h + 64], lhsT=QM[:, tt, h, 64 * hf:64 * hf + 64], rhs=SBF[:, 2 * tt + hf, 64 * h:64 * h + 64],
                                                                      start=False, stop=False), reads=[QM.b, SBF.b], writes=[po.b], inc=False)
                                S.op("tensor", lambda e: e.matmul(po[:, 64 * h:64 * h + 64], lhsT=am[:, h, :], rhs=GV[:, t, 64 * h:64 * h + 64], start=False, stop=(h == 3)),
                                     reads=[am.b, GV.b], writes=[po.b], inc=(h == 3))
                            osb, og, rs = OSB[tt % 2], OG[tt % 2], RS[tt % 2]
                            S.op("vector", lambda e: e.tensor_copy(osb[:], po[:, 0:256]), reads=[po.b], writes=[osb.b])
                            S.op("vector", lambda e: e.tensor_tensor(SQ[:], osb[:], osb[:], ALU.mult), reads=[osb.b], writes=[SQ.b])
                            S.op("vector", lambda e: e.tensor_reduce(rs[:, 0:4], SQ[:].rearrange("p (h d) -> p h d", h=4), AX.X, ALU.add), reads=[SQ.b], writes=[rs.b])
                            S.op("scalar", lambda e: e.activation(rs[:, 4:8], rs[:, 0:4], AF.Sqrt, bias=epsc[:, 1:2], scale=1.0 / 64.0), reads=[rs.b, epsc.b], writes=[rs.b])
                            S.op("vector", lambda e: e.reciprocal(rs[:, 4:8], rs[:, 4:8]), reads=[rs.b], writes=[rs.b])
                            for h in range(4):
                                S.op("vector", lambda e: e.scalar_tensor_tensor(out=og[:, 64 * h:64 * h + 64], in0=osb[:, 64 * h:64 * h + 64], scalar=rs[:, 4 + h:5 + h],
                                                                                in1=NGSR[:, t, 64 * h:64 * h + 64], op0=ALU.mult, op1=ALU.mult), reads=[osb.b, rs.b, NGSR.b], writes=[og.b])
                            pt_ = bank()
                            for b2 in range(2):
                                S.op("tensor", lambda e: e.transpose(pt_[:, b2 * 128:(b2 + 1) * 128], og[:, b2 * 128:(b2 + 1) * 128], ident[:]), reads=[og.b, ident.b], writes=[pt_.b], inc=(b2 == 1))
                            evac(AT[:, 6:8, t * 128:(t + 1) * 128], pt_[:, 0:256].rearrange("p (c k) -> p c k", c=2), [pt_.b], [AT.bs[t]])
                        S.op("vector", lambda e: e.tensor_copy(SBF[:, 0, :], SBF[:, 8, :]), reads=[SBF.b], writes=[SBF.b])
                    S.barrier()

                if stage == "gla":
                    return
                dump("mixT", AT[:], all_at)
                nodma[0] = (stage == "lnnodma")
                if stage != "mm":
                    prefetch_res(0, False)
                pend = None
                for t in range(NT):
                    yb = [bank(), bank()]
                    for dh in range(2):
                        for k in range(8):
                            S.op("tensor", lambda e: e.matmul(yb[dh][:], lhsT=AT[:, k, t * 128:(t + 1) * 128], rhs=WO[dh][:, k, :], start=(k == 0), stop=(k == 7)),
                                 reads=[AT.bs[t], WO[dh].b], writes=[yb[dh].b], inc=(k == 7))
                    if stage == "mm":
                        continue
                    if pend is not None:
                        ln_post(*pend)
                    pend = (t, ln_tile(t, yb, 1.0 / ALPHA, False), False)
                    if t < NT - 1:
                        prefetch_res(t + 1, False)
                if pend is not None:
                    ln_post(*pend)
                nodma[0] = False
                S.barrier()

        for l in range(depth):
            ffn(l, 0, first=(l == 0), last=False)
            if mix:
                mixer(l)
            ffn(l, 1, first=False, last=(l == depth - 1))

        for c in d_xout:
            S._wait(S.E["sync"], (c, c.cnt))
        print("instructions:", S.n_ins)
    return nc


_NC_CACHE = {}


LAUNCH_DEPTH = 4


def kernel(**inputs):
    x = np.ascontiguousarray(inputs["x"], dtype=np.float32)
    B = x.shape[0]
    key = ("nc", LAUNCH_DEPTH)
    if key not in _NC_CACHE:
        _NC_CACHE[key] = build(LAUNCH_DEPTH)
    nc = _NC_CACHE[key]
    cst = consts()
    cur = x
    for l0 in range(0, DEPTH, LAUNCH_DEPTH):
        shared = {k: np.ascontiguousarray(v[l0:l0 + LAUNCH_DEPTH], dtype=np.float32) for k, v in inputs.items() if k != "x"}
        shared.update(cst)
        in_maps = []
        for b in range(B):
            m = dict(shared)
            m["x"] = np.ascontiguousarray(cur[b])
            in_maps.append(m)
        res = run_bass_kernel_spmd(nc, in_maps, core_ids=list(range(B)))
        cur = np.stack([r["y"] for r in res.results], axis=0)
    return cur
```

```python
import contextlib
import os
VAR = os.environ.get('KVAR', '2')
import numpy as np
import concourse.bass as bass
import concourse.mybir as mybir
from concourse.bass_utils import run_bass_kernel_spmd

F32 = mybir.dt.float32
BF16 = mybir.dt.bfloat16
AF = mybir.ActivationFunctionType
ALU = mybir.AluOpType
AX = mybir.AxisListType

S_LEN = 2048
D = 1024
DFF = 2816
NFF = DFF // 128
NT = S_LEN // 128
DIN = 2344
DEPTH = 4
ALPHA = (2.0 * DEPTH) ** 0.25
LN_EPS = 1e-5
RMS_EPS = 1e-6
NEG = -30000.0


CONST_SHAPES = {
    "c_causal": (128, 128), "c_anti": (128, 128), "c_gmask4": (128, 4, 128), "c_cmpmask": (128, 4, 512),
    "c_onehot2": (128, 2048), "c_pb": (128, 3, 4, 128), "c_keep": (128, 16, 32), "c_add": (128, 16, 32),
    "c_hm": (128, 4), "c_scanmask": (128, 512), "c_ovl": (128, 32),
}


def consts():
    c = {"c_ident": np.eye(128, dtype=np.float32)}
    k = np.arange(128)[:, None]
    q = np.arange(128)[None, :]
    c["c_causal"] = (k <= q).astype(np.float32)
    c["c_anti"] = (k > q).astype(np.float32)
    gm = ((k // 64 == q // 64) & (k <= q)).astype(np.float32)
    c["c_gmask4"] = np.repeat(gm[:, None, :], 4, axis=1)
    n = np.arange(128)[:, None, None]
    t = (np.arange(4)[None, :, None] * 512 + np.arange(512)[None, None, :])
    c["c_cmpmask"] = ((16 * n + 31 <= t) & (n < 127)).astype(np.float32)
    oh = np.zeros((128, 2048), np.float32)
    key = np.arange(2048)
    for m in range(32):
        oh[m, key // 64 == m] = 1.0
        oh[64 + m, key // 64 == m] = 1.0
    c["c_onehot2"] = oh
    pb = np.zeros((128, 3, 4, 128), np.float32)
    tp = np.arange(128)[:, None]
    tt = np.arange(128)[None, :]
    for gi, w in enumerate((2, 4, 8, 16)):
        cur = ((tp <= tt) & (tp > tt - w)).astype(np.float32) / w
        pb[:, 0, gi, :] = cur - (tp == tt)
        prev = ((tp - 128 <= tt) & (tp - 128 > tt - w)).astype(np.float32) / w
        pb[:, 1, gi, :] = prev
        cnt = np.minimum(w, tt + 1).astype(np.float32)
        pb[:, 2, gi, :] = ((tp <= tt) & (tp > tt - w)).astype(np.float32) / cnt - (tp == tt)
    c["c_pb"] = pb
    tok = (np.arange(16)[None, :, None] * 128 + np.arange(128)[:, None, None])
    m = np.arange(32)[None, None, :]
    cur = tok // 64
    forced = (m == 0) | (m == cur) | (m == cur - 1)
    future = (m * 64 > tok)
    c["c_keep"] = (~forced & ~future).astype(np.float32)
    c["c_add"] = np.where(forced, 1e4, np.where(future, -1e4, 0.0)).astype(np.float32)
    f = np.arange(128)[:, None]
    c["c_hm"] = ((f // 32) == np.arange(4)[None, :]).astype(np.float32) * (32 ** -0.5)
    sm = np.ones((128, 512), np.float32)
    sm[:, ::64] = 0.0
    c["c_scanmask"] = sm
    nn = np.arange(128)[:, None]
    mm = np.arange(32)[None, :]
    c["c_ovl"] = ((16 * nn <= 64 * mm + 63) & (16 * nn + 31 >= 64 * mm) & (nn < 127)).astype(np.float32)
    return c


class Ctr:
    _n = 0

    def __init__(self, sem, name):
        self.sem = sem
        self.cnt = 0
        self.name = name
        self.id = Ctr._n
        Ctr._n += 1


class Eng:
    def __init__(self, eng, ctr, name):
        self.eng = eng
        self.ctr = ctr
        self.name = name
        self.seen = {}
        self.pending = 0


class Buf:
    __slots__ = ("name", "w", "r")

    def __init__(self, name):
        self.name = name
        self.w = None
        self.r = []


class Sched:
    def __init__(self, nc, stack):
        self.nc = nc
        self.stack = stack
        self.E = {}
        for nm in ("tensor", "vector", "scalar", "gpsimd", "sync"):
            sem = stack.enter_context(nc.semaphore("e_" + nm))
            self.E[nm] = Eng(getattr(nc, nm), Ctr(sem, nm), nm)
        self.dma_ctrs = []
        self.n_ins = 0
        self.outq = {}

    def dma_ctr(self, name):
        sem = self.stack.enter_context(self.nc.semaphore("d_" + name))
        c = Ctr(sem, name)
        self.dma_ctrs.append(c)
        return c

    def _wait(self, e, tok):
        ctr, val = tok
        if ctr is e.ctr and e.name == "tensor":
            return
        if e.seen.get(ctr.id, 0) >= val:
            return
        e.eng.wait_ge(ctr.sem, val)
        e.seen[ctr.id] = val
        self.n_ins += 1

    def _deps(self, e, reads, writes, skip=None):
        for b in reads:
            if b.w is not None:
                self._wait(e, b.w)
        for b in writes:
            if b.w is not None and b.w[0] is not skip:
                self._wait(e, b.w)
            for t in b.r:
                self._wait(e, t)

    def _record(self, tok, reads, writes):
        for b in reads:
            b.r.append(tok)
            if len(b.r) > 64:
                d = {}
                for c, v in b.r:
                    if c.id not in d or d[c.id][1] < v:
                        d[c.id] = (c, v)
                b.r = list(d.values())
        for b in writes:
            b.w = tok
            b.r = []

    def op(self, en, fn, reads=(), writes=(), inc=True):
        e = self.E[en]
        self._deps(e, reads, writes)
        ins = fn(e.eng)
        self.n_ins += 1
        if inc:
            e.ctr.cnt += 1
            ins.then_inc(e.ctr.sem, 1)
            tok = (e.ctr, e.ctr.cnt)
            e.pending = 0
        else:
            tok = (e.ctr, e.ctr.cnt + 1)
            e.pending += 1
        self._record(tok, reads, writes)
        return ins

    MAX_OUT = {"gpsimd": 4, "sync": 6, "scalar": 4}

    def dma(self, qn, ctr, out, in_, reads=(), writes=()):
        e = self.E[qn]
        q = self.outq.setdefault(qn, [])
        if len(q) >= self.MAX_OUT.get(qn, 4):
            c0, v0 = q.pop(0)
            if v0 == c0.cnt:
                self._wait(e, (c0, v0))
        self._deps(e, reads, writes, skip=ctr)
        ins = e.eng.dma_start(out=out, in_=in_)
        ctr.cnt += 16
        ins.then_inc(ctr.sem, 16)
        self.n_ins += 1
        q.append((ctr, ctr.cnt))
        self._record((ctr, ctr.cnt), reads, writes)

    def barrier(self):
        for e in self.E.values():
            assert e.pending == 0, e.name
        for e in self.E.values():
            for f in self.E.values():
                if f is not e and f.ctr.cnt > 0:
                    self._wait(e, (f.ctr, f.ctr.cnt))
            for c in self.dma_ctrs:
                if c.cnt > 0:
                    self._wait(e, (c, c.cnt))


class Tile:
    def __init__(self, t, name, nbuf=1):
        self.t = t
        self.b = Buf(name)
        self.bs = [Buf(f"{name}{i}") for i in range(nbuf)]

    def __getitem__(self, k):
        return self.t[k]


def build(depth=DEPTH, dbg=None, mix=True, stage=None, stage_from=0):
    nc = bass.Bass("TRN2", target_bir_lowering=False)
    dt = nc.dram_tensor
    L = depth
    x_d = dt("x", [S_LEN, D], F32, kind="ExternalInput").ap()
    ln_g = dt("ln_g", [L, 3, D], F32, kind="ExternalInput").ap()
    ln_b = dt("ln_b", [L, 3, D], F32, kind="ExternalInput").ap()
    wg_d = dt("ffn_wg", [L, 2, D, DFF], F32, kind="ExternalInput").ap()
    wu_d = dt("ffn_wu", [L, 2, D, DFF], F32, kind="ExternalInput").ap()
    wd_d = dt("ffn_wd", [L, 2, DFF, D], F32, kind="ExternalInput").ap()
    y_d = dt("y", [S_LEN, D], F32, kind="ExternalOutput").ap()
    xd = y_d if os.environ.get("KXD", "y") == "y" else dt("xd_scratch", [S_LEN, D], F32, kind="Internal").ap()
    ident_d = dt("c_ident", [128, 128], F32, kind="ExternalInput").ap()
    w_in_d = dt("w_in", [L, D, DIN], F32, kind="ExternalInput").ap()
    w_out_d = dt("w_out", [L, D, D], F32, kind="ExternalInput").ap()
    cmp_pe_d = dt("cmp_pe", [L, 2, 32, 64], F32, kind="ExternalInput").ap()
    cmp_w1_d = dt("cmp_w1", [L, 2, 2048, 64], F32, kind="ExternalInput").ap()
    cmp_w2_d = dt("cmp_w2", [L, 2, 64, 64], F32, kind="ExternalInput").ap()
    pool_w_d = dt("pool_w", [L, 4, 64, 64], F32, kind="ExternalInput").ap()
    pool_scale_d = dt("pool_scale", [L, 256], F32, kind="ExternalInput").ap()
    wa2_d = dt("gla_wa2", [L, 16, 128], F32, kind="ExternalInput").ap()
    ba_d = dt("gla_ba", [L, 128], F32, kind="ExternalInput").ap()
    ng_d = dt("gla_norm_g", [L, 256], F32, kind="ExternalInput").ap()
    cd = {}
    for nm, shp in CONST_SHAPES.items():
        cd[nm] = dt(nm, list(shp), F32, kind="ExternalInput").ap()
    dbg_out = {}
    if dbg:
        for nm, (shp, dty) in dbg.items():
            dbg_out[nm] = dt("dbg_" + nm, list(shp), dty, kind="ExternalOutput").ap()

    stack = contextlib.ExitStack()
    with stack:
        S = Sched(nc, stack)

        def sb(name, shape, dtype, nbuf=1):
            t = stack.enter_context(nc.sbuf_tensor(name, shape, dtype))
            return Tile(t, name, nbuf)

        AT = sb("AT", [128, 8, S_LEN], BF16, nbuf=NT)
        GB = sb("GB", [128, 2, D], F32)
        ident = sb("ident", [128, 128], F32)
        xres = [sb(f"xres{i}", [128, D], F32) for i in range(2)]
        xout = [sb(f"xout{i}", [128, D], F32) for i in range(2)]
        lnst = [sb(f"lnst{i}", [128, 16], F32) for i in range(2)]
        banks = []
        for i in range(8):
            p = stack.enter_context(nc.psum_tensor(f"ps{i}", [128, 512], F32))
            banks.append(Tile(p, f"ps{i}"))
        bank_rr = [0]

        def bank():
            b = banks[bank_rr[0] % 6]
            bank_rr[0] += 1
            return b
        acc_rr = [0]

        def acc_bank():
            b = banks[6 + acc_rr[0] % 2]
            acc_rr[0] += 1
            return b

        d_xres = [S.dma_ctr(f"xres{i}") for i in range(2)]
        d_xout = [S.dma_ctr(f"xout{i}") for i in range(2)]
        d_misc = S.dma_ctr("misc")
        d_gb = S.dma_ctr("gb")
        d_w = [S.dma_ctr(f"w{i}") for i in range(2)]
        d_wu = [S.dma_ctr(f"wu{i}") for i in range(2)]
        d_wd = S.dma_ctr("wd")
        XD = [Buf(f"xd{t}") for t in range(NT)]

        S.dma("sync", d_misc, ident[:], ident_d, writes=[ident.b])
        epsc = sb("epsc", [128, 2], F32)
        S.op("vector", lambda e: e.memset(epsc[:, 0:1], LN_EPS / (ALPHA * ALPHA)), writes=[epsc.b])
        S.op("vector", lambda e: e.memset(epsc[:, 1:2], RMS_EPS), writes=[epsc.b])

        ln_ctr = [0]

        def transposes_to_AT(src, t):
            for half in range(2):
                pb = bank()
                for c in range(4):
                    cc = half * 4 + c
                    S.op("tensor", lambda e: e.transpose(pb[:, c * 128:(c + 1) * 128], src[:, cc * 128:(cc + 1) * 128], ident[:]),
                         reads=[src.b, ident.b], writes=[pb.b], inc=(c == 3))
                dst = AT[:, half * 4:half * 4 + 4, t * 128:(t + 1) * 128]
                srcv = pb[:].rearrange("p (c k) -> p c k", c=4)
                if half == 0:
                    S.op("scalar", lambda e: e.copy(dst, srcv), reads=[pb.b], writes=[AT.bs[t]])
                else:
                    S.op("vector", lambda e: e.tensor_copy(dst, srcv), reads=[pb.b], writes=[AT.bs[t]])

        def load_gb(l, s):
            S.dma("sync", d_gb, GB[:, 0, :], ln_g[l, s].partition_broadcast(128), writes=[GB.b])
            S.dma("sync", d_gb, GB[:, 1, :], ln_b[l, s].partition_broadcast(128), writes=[GB.b])

        def prefetch_res(t, first):
            if nodma[0]:
                return
            i = ln_ctr[0] % 2
            src = (x_d if first else xd)[t * 128:(t + 1) * 128, :]
            S.dma("sync", d_xres[i], xres[i][:], src, reads=[XD[t]], writes=[xres[i].b])

        nodma = [False]

        def ln_tile(t, ybanks, cscale, last):
            i = ln_ctr[0] % 2
            ln_ctr[0] += 1
            xr, xo, z, st = xres[i], xout[i], xout[i], lnst[i]
            for h in range(2):
                S.op("vector", lambda e: e.scalar_tensor_tensor(out=z[:, h * 512:(h + 1) * 512], in0=ybanks[h][:], scalar=cscale,
                                                                in1=xr[:, h * 512:(h + 1) * 512], op0=ALU.mult, op1=ALU.add),
                     reads=[ybanks[h].b, xr.b], writes=[z.b])
            for h in range(2):
                S.op("vector", lambda e: e.bn_stats(st[:, h * 6:(h + 1) * 6], z[:, h * 512:(h + 1) * 512]), reads=[z.b], writes=[st.b])
            S.op("vector", lambda e: e.bn_aggr(st[:, 12:14], st[:, 0:12]), reads=[st.b], writes=[st.b])
            S.op("scalar", lambda e: e.activation(st[:, 14:15], st[:, 13:14], AF.Sqrt, bias=epsc[:, 0:1], scale=1.0),
                 reads=[st.b, epsc.b], writes=[st.b])
            S.op("vector", lambda e: e.reciprocal(st[:, 14:15], st[:, 14:15]), reads=[st.b], writes=[st.b])
            S.op("vector", lambda e: e.scalar_tensor_tensor(out=st[:, 15:16], in0=st[:, 12:13], scalar=-1.0, in1=st[:, 14:15],
                                                            op0=ALU.mult, op1=ALU.mult), reads=[st.b], writes=[st.b])
            S.op("scalar", lambda e: e.activation(z[:], z[:], AF.Identity, bias=st[:, 15:16], scale=st[:, 14:15]),
                 reads=[z.b, st.b], writes=[z.b])
            S.op("vector", lambda e: e.tensor_tensor(xo[:], z[:], GB[:, 0, :], ALU.mult), reads=[z.b, GB.b], writes=[xo.b])
            S.op("vector", lambda e: e.tensor_tensor(xo[:], xo[:], GB[:, 1, :], ALU.add), reads=[xo.b, GB.b], writes=[xo.b])
            return i

        def ln_post(t, i, last):
            xo = xout[i]
            if not last:
                transposes_to_AT(xo, t)
            dst = (y_d if last else xd)[t * 128:(t + 1) * 128, :]
            if not nodma[0]:
                S.dma("sync", d_xout[i], dst, xo[:], reads=[xo.b], writes=[XD[t]])

        for t in range(NT):
            i = t % 2
            S.dma("sync", d_xres[i], xres[i][:], x_d[t * 128:(t + 1) * 128, :], writes=[xres[i].b])
            transposes_to_AT(xres[i], t)

        def ffn(l, s, first, last):
            with contextlib.ExitStack() as fs:
                def fsb(name, shape, dtype, nbuf=1):
                    return Tile(fs.enter_context(nc.sbuf_tensor(f"{name}_{l}_{s}", shape, dtype)), name, nbuf)
                WD = fsb("WD", [128, NFF, D], BF16, nbuf=NFF)
                HT = fsb("HT", [128, NFF, 1024], BF16, nbuf=NFF * 2)
                WS = [fsb(f"WS{i}", [128, 2, 8, 256], BF16, nbuf=2) for i in range(2)]
                SG = [fsb(f"SG{i}", [128, 512], F32) for i in range(2)]
                load_gb(l, 2 if s == 1 else 0)
                wgl, wul, wdl = wg_d[l, s], wu_d[l, s], wd_d[l, s]
                wjobs = [(H, cg) for H in range(2) for cg in range(NFF // 2)]

                def issue_w(j):
                    H, cg = wjobs[j]
                    w = WS[j % 2]
                    S.dma("gpsimd", d_w[j % 2], w[:, 0], wgl[:, cg * 256:(cg + 1) * 256].rearrange("(k p) c -> p k c", p=128), writes=[w.bs[0]])
                    S.dma("gpsimd", d_wu[j % 2], w[:, 1], wul[:, cg * 256:(cg + 1) * 256].rearrange("(k p) c -> p k c", p=128), writes=[w.bs[1]])

                issue_w(0)
                issue_w(1)
                for c in range(0, NFF, 2):
                    S.dma("gpsimd", d_wd, WD[:, c:c + 2, :], wdl[c * 128:(c + 2) * 128, :].rearrange("(c p) d -> p c d", p=128),
                          writes=[WD.bs[c], WD.bs[c + 1]])
                for c in range(NFF):
                    WD.bs[c].w = (d_wd, d_wd.cnt)
                sgi = 0
                for H in range(2):
                    for cg in range(NFF // 2):
                        j = H * (NFF // 2) + cg
                        w = WS[j % 2]
                        for c2 in range(2):
                            ch = cg * 2 + c2
                            for tg in range(2):
                                tok0 = H * 1024 + tg * 512
                                atb = [AT.bs[tok0 // 128 + q] for q in range(4)]
                                pg, pu = bank(), bank()
                                for k in range(8):
                                    S.op("tensor", lambda e: e.matmul(pg[:], lhsT=w[:, 0, k, c2 * 128:(c2 + 1) * 128], rhs=AT[:, k, tok0:tok0 + 512],
                                                                      start=(k == 0), stop=(k == 7)),
                                         reads=[w.bs[0]] + atb, writes=[pg.b], inc=(k == 7))
                                for k in range(8):
                                    S.op("tensor", lambda e: e.matmul(pu[:], lhsT=w[:, 1, k, c2 * 128:(c2 + 1) * 128], rhs=AT[:, k, tok0:tok0 + 512],
                                                                      start=(k == 0), stop=(k == 7)),
                                         reads=[w.bs[1]] + atb, writes=[pu.b], inc=(k == 7))
                                sg = SG[sgi % 2]
                                sgi += 1
                                S.op("scalar", lambda e: e.activation(sg[:], pg[:], AF.Silu), reads=[pg.b], writes=[sg.b])
                                hb = HT.bs[ch * 2 + tg]
                                S.op("vector", lambda e: e.tensor_tensor(HT[:, ch, tg * 512:(tg + 1) * 512], sg[:], pu[:], ALU.mult),
                                     reads=[sg.b, pu.b], writes=[hb])
                        if j + 2 < len(wjobs):
                            issue_w(j + 2)
                    prefetch_res(H * 8, first)
                    pend = None
                    for tt in range(8):
                        t = H * 8 + tt
                        yb = [bank(), bank()]
                        for dh in range(2):
                            for c in range(NFF):
                                S.op("tensor", lambda e: e.matmul(yb[dh][:], lhsT=HT[:, c, tt * 128:(tt + 1) * 128], rhs=WD[:, c, dh * 512:(dh + 1) * 512],
                                                                  start=(c == 0), stop=(c == NFF - 1)),
                                     reads=[HT.bs[c * 2 + tt // 4], WD.bs[c]], writes=[yb[dh].b], inc=(c == NFF - 1))
                        if pend is not None:
                            ln_post(*pend)
                        pend = (t, ln_tile(t, yb, 0.5 / ALPHA, last), last)
                        if tt < 7:
                            prefetch_res(t + 1, first)
                    ln_post(*pend)
                S.barrier()

        def cload(nm, shape, dtype):
            t = sb("s_" + nm, list(shape), dtype)
            d_misc2 = S.dma_ctr("k_" + nm)
            if dtype == F32:
                S.dma("sync", d_misc2, t[:], cd[nm], writes=[t.b])
            else:
                nd = len(shape)
                if nd == 2:
                    tv, cv = t[:], cd[nm]
                else:
                    names = " ".join("abcd"[:nd - 1])
                    pat = f"p {names} -> p ({names})"
                    tv, cv = t[:].rearrange(pat), cd[nm].rearrange(pat)
                n = int(np.prod(shape[1:]))
                for c0 in range(0, n, 512):
                    c1 = min(n, c0 + 512)
                    S.dma("gpsimd", d_misc2, tv[:, c0:c1], cv[:, c0:c1], writes=[t.b])
            return t
        lw_ctrs = {}

        def d_lw_for(tile):
            if tile.b.name not in lw_ctrs:
                lw_ctrs[tile.b.name] = S.dma_ctr("lw_" + tile.b.name)
            return lw_ctrs[tile.b.name]
        causal = cload("c_causal", (128, 128), BF16)
        anti = cload("c_anti", (128, 128), BF16)
        gmask4 = cload("c_gmask4", (128, 4, 128), BF16)
        cmpmask = cload("c_cmpmask", (128, 4, 512), BF16)
        onehot2 = cload("c_onehot2", (128, 2048), BF16)
        pbc = cload("c_pb", (128, 3, 4, 128), BF16)
        keepm = cload("c_keep", (128, 16, 32), F32)
        addm = cload("c_add", (128, 16, 32), F32)
        hm = cload("c_hm", (128, 4), F32)
        scanmask = cload("c_scanmask", (128, 512), F32)
        ovl = cload("c_ovl", (128, 32), F32)
        zeros = sb("zeros", [128, 512], BF16)
        S.op("vector", lambda e: e.memset(zeros[:], 0.0), writes=[zeros.b])
        SELS = sb("SELS", [128, 4, 128], F32)
        S.op("vector", lambda e: e.memset(SELS[:], 0.0), writes=[SELS.b])

        def dump(nm, ap, rd):
            if nm in dbg_out:
                S.dma("sync", d_misc, dbg_out[nm], ap, reads=rd)

        def zero_bank(pb, m, n):
            S.op("tensor", lambda e: e.matmul(pb[0:m, 0:n], lhsT=zeros[0:1, 0:m], rhs=zeros[0:1, 0:n], start=True, stop=False),
                 reads=[zeros.b], writes=[pb.b], inc=False)

        def mixer(l, stage=stage):
            if l < stage_from:
                stage = None
            with contextlib.ExitStack() as ms:
                def msb(name, shape, dtype, nbuf=1):
                    return Tile(ms.enter_context(nc.sbuf_tensor(f"{name}_m{l}", shape, dtype)), name, nbuf)
                win = w_in_d[l]
                GQK = msb("GQK", [128, 2, S_LEN], BF16)
                GA = msb("GA", [16, S_LEN], BF16)
                UT = msb("UT", [128, NT, 256], BF16)
                GV = msb("GV", [128, NT, 256], BF16)
                NGSR = msb("NGSR", [128, NT, 256], BF16)
                WM = [msb(f"WM{i}", [128, 8, 512], BF16) for i in range(2)]
                NG = msb("NG", [128, 256], F32)
                PSC = msb("PSC", [128, 2], F32)
                BA = msb("BA", [128, 1], F32)
                WA2 = msb("WA2", [16, 128], BF16)
                PWP = msb("PWP", [64, 4, 128], BF16)
                SR = [msb(f"SR{i}", [128, 256], F32) for i in range(2)]
                nst = contextlib.ExitStack()

                def nsb0(name, shape, dtype):
                    return Tile(nst.enter_context(nc.sbuf_tensor(f"{name}_m{l}", shape, dtype)), name)
                QT2 = nsb0("QT2", [128, 4, S_LEN], BF16)
                KS2 = nsb0("KS2", [128, 2, S_LEN], BF16)
                KW2 = nsb0("KW2", [128, 2, S_LEN], BF16)
                KCV = nsb0("KCV", [128, 2, S_LEN], BF16)
                VSA = nsb0("VSA", [128, NT, 2, 65], BF16)
                VWA = nsb0("VWA", [128, NT, 2, 65], BF16)
                GATES = nsb0("GATES", [128, NT, 24], F32)
                SB2 = nsb0("SB2", [128, 2, S_LEN], BF16)
                KCMP2 = nsb0("KCMP2", [128, 2, 128], BF16)
                VCA = nsb0("VCA", [128, 2, 97], BF16)
                load_gb(l, 1)
                S.dma("sync", d_lw_for(NG), NG[:], ng_d[l].partition_broadcast(128), writes=[NG.b])
                for pr in range(2):
                    S.dma("sync", d_lw_for(PSC), PSC[:, pr:pr + 1], pool_scale_d[l, pr * 128:(pr + 1) * 128].rearrange("(p o) -> p o", o=1), writes=[PSC.b])
                S.dma("sync", d_lw_for(BA), BA[:], ba_d[l].rearrange("(p o) -> p o", o=1), writes=[BA.b])
                S.dma("gpsimd", d_lw_for(WA2), WA2[:], wa2_d[l], writes=[WA2.b])
                S.op("vector", lambda e: e.memset(PWP[:], 0.0), writes=[PWP.b])
                for g in range(4):
                    S.dma("gpsimd", d_lw_for(PWP), PWP[:, g, (g % 2) * 64:(g % 2) * 64 + 64], pool_w_d[l, g], writes=[PWP.b])
                S.op("vector", lambda e: e.memset(VSA[:, :, :, 64:65], 1.0), writes=[VSA.b])
                S.op("vector", lambda e: e.memset(VWA[:, :, :, 64:65], 1.0), writes=[VWA.b])
                S.op("vector", lambda e: e.memset(VCA[:], 0.0), writes=[VCA.b])
                S.op("vector", lambda e: e.memset(KCMP2[:], 0.0), writes=[KCMP2.b])
                S.op("vector", lambda e: e.memset(VCA[:, :, 64:65], 1.0), writes=[VCA.b])
                for g in range(2):
                    S.op("vector", lambda e: e.tensor_copy(VCA[:, g, 65:97], ovl[:]), reads=[ovl.b], writes=[VCA.b])

                def load_w(slot, specs):
                    w = WM[slot]
                    for (d0, s0, n) in specs:
                        S.dma("gpsimd", d_w[slot], w[:, :, d0:d0 + n], win[:, s0:s0 + n].rearrange("(k p) c -> p k c", p=128), writes=[w.b])
                    return w

                all_at = list(AT.bs)
                ev_rr = [0]

                def evac(dst, src, rd, wr, scale=None):
                    ev_rr[0] += 1
                    if scale is not None:
                        S.op("scalar", lambda e: e.mul(dst, src, scale), reads=rd, writes=wr)
                    elif ev_rr[0] % 2 == 0:
                        S.op("scalar", lambda e: e.copy(dst, src), reads=rd, writes=wr)
                    else:
                        S.op("vector", lambda e: e.tensor_copy(dst, src), reads=rd, writes=wr)

                def fm_job(w, chunks):
                    for tg in range(4):
                        for (c0, m, fn) in chunks:
                            pb = bank()
                            for k in range(8):
                                S.op("tensor", lambda e: e.matmul(pb[0:m, :], lhsT=w[:, k, c0:c0 + m], rhs=AT[:, k, tg * 512:(tg + 1) * 512],
                                                                  start=(k == 0), stop=(k == 7)),
                                     reads=[w.b] + all_at[tg * 4:tg * 4 + 4], writes=[pb.b], inc=(k == 7))
                            fn(pb, tg)

                def tok_job(w, ncols, fn, wc0=0):
                    for t in range(NT):
                        pb = bank()
                        for k in range(8):
                            S.op("tensor", lambda e: e.matmul(pb[:, 0:ncols], lhsT=AT[:, k, t * 128:(t + 1) * 128], rhs=w[:, k, wc0:wc0 + ncols],
                                                              start=(k == 0), stop=(k == 7)),
                                 reads=[w.b, AT.bs[t]], writes=[pb.b], inc=(k == 7))
                        fn(pb, t)

                tsl = lambda tg: slice(tg * 512, (tg + 1) * 512)
                w = load_w(0, [(0, 0, 512)])
                w2 = load_w(1, [(0, 768, 64), (64, 768, 64), (128, 832, 64), (192, 832, 64),
                                (256, 1024, 64), (320, 1024, 64), (384, 1088, 64), (448, 1088, 64)])
                fm_job(w, [(p * 128, 128, (lambda pb, tg, p=p: evac(QT2[:, p, tsl(tg)], pb[:], [pb.b], [QT2.b], scale=0.125))) for p in range(4)])
                dsts = [(KS2, 0), (KS2, 1), (KW2, 0), (KW2, 1)]
                fm_job(w2, [(i * 128, 128, (lambda pb, tg, i=i: evac(dsts[i][0][:, dsts[i][1], tsl(tg)], pb[:], [pb.b], [dsts[i][0].b]))) for i in range(4)])
                if stage == "j2":
                    S.barrier(); nst.close(); return
                w = load_w(0, [(0, 512, 256), (256, 1560, 256)])
                dst3 = [(KCV, 0), (KCV, 1), (GQK, 0), (GQK, 1)]
                fm_job(w, [(i * 128, 128, (lambda pb, tg, i=i: evac(dst3[i][0][:, dst3[i][1], tsl(tg)], pb[:], [pb.b], [dst3[i][0].b]))) for i in range(4)])
                if stage == "j3":
                    S.barrier(); nst.close(); return
                w2 = load_w(1, [(0, 896, 128), (128, 1152, 152)])

                def ev5(pb, t):
                    S.op("vector", lambda e: e.tensor_copy(VSA[:, t, :, 0:64], pb[:, 0:128].rearrange("p (g d) -> p g d", g=2)), reads=[pb.b], writes=[VSA.b])
                    S.op("vector", lambda e: e.tensor_copy(VWA[:, t, :, 0:64], pb[:, 128:256].rearrange("p (g d) -> p g d", g=2)), reads=[pb.b], writes=[VWA.b])
                    S.op("scalar", lambda e: e.activation(GATES[:, t, :], pb[:, 256:280], AF.Sigmoid), reads=[pb.b], writes=[GATES.b])
                tok_job(w2, 280, ev5)
                if stage == "j5":
                    S.barrier(); nst.close(); return
                w = load_w(1 if VAR == '1' else 0, [(0, 1304, 256), (256, 1816, 256)])

                def ev6(pb, t):
                    S.op("vector" if VAR == '2' else "scalar", (lambda e: e.tensor_copy(UT[:, t, :], pb[:, 0:256])) if VAR == '2' else (lambda e: e.copy(UT[:, t, :], pb[:, 0:256])), reads=[pb.b], writes=[UT.b])
                    S.op("vector", lambda e: e.tensor_copy(GV[:, t, :], pb[:, 256:512]), reads=[pb.b], writes=[GV.b])
                tok_job(w, 512, ev6)
                if stage == "j6":
                    S.barrier(); nst.close(); return
                w2 = load_w(1, [(0, 2072, 272)])

                def ev7(pb, t):
                    sr = SR[t % 2]
                    S.op("scalar", lambda e: e.activation(sr[:], pb[:, 0:256], AF.Silu), reads=[pb.b], writes=[sr.b])
                    S.op("vector", lambda e: e.tensor_tensor(NGSR[:, t, :], sr[:], NG[:], ALU.mult), reads=[sr.b, NG.b], writes=[NGSR.b])
                tok_job(w2, 256, ev7, wc0=16)
                fm_job(w2, [(0, 16, (lambda pb, tg: evac(GA[0:16, tsl(tg)], pb[0:16, :], [pb.b], [GA.b])))])
                WO = []
                for i in range(2):
                    wo = WM[i]
                    S.dma("gpsimd", d_w[i], wo[:], w_out_d[l][:, i * 512:(i + 1) * 512].rearrange("(k p) c -> p k c", p=128), writes=[wo.b])
                    WO.append(wo)

                if stage == "proj":
                    S.barrier(); nst.close(); return
                with contextlib.ExitStack() as cs:
                    def csb(name, shape, dtype):
                        return Tile(cs.enter_context(nc.sbuf_tensor(f"{name}_c{l}", shape, dtype)), name)
                    W1 = csb("W1", [128, 2, 32, 64], BF16)
                    PEs = csb("PEs", [32, 2, 64], F32)
                    PET = csb("PET", [64, 2, 32], BF16)
                    CB = csb("CB", [64, 2], F32)
                    W2K = csb("W2K", [64, 128], BF16)
                    W2V = csb("W2V", [64, 64], BF16)
                    GX = csb("GX", [64, 128], F32)
                    GT = csb("GT", [64, 128], F32)
                    GEL = csb("GEL", [64, 128], BF16)
                    for kv in range(2):
                        for hf in range(2):
                            w1v = cmp_w1_d[l, kv].rearrange("(j d) o -> d j o", d=64)
                            for j0 in range(0, 32, 8):
                                S.dma("gpsimd", d_lw_for(W1), W1[hf * 64:(hf + 1) * 64, kv, j0:j0 + 8, :], w1v[:, j0:j0 + 8, :], writes=[W1.b])
                        S.dma("sync", d_lw_for(PEs), PEs[:, kv, :], cmp_pe_d[l, kv], writes=[PEs.b])
                    S.dma("gpsimd", d_lw_for(W2K), W2K[:, 0:64], cmp_w2_d[l, 0], writes=[W2K.b])
                    S.dma("gpsimd", d_lw_for(W2K), W2K[:, 64:128], cmp_w2_d[l, 0], writes=[W2K.b])
                    S.dma("gpsimd", d_lw_for(W2V), W2V[:], cmp_w2_d[l, 1], writes=[W2V.b])
                    for kv in range(2):
                        pb = bank()
                        S.op("tensor", lambda e: e.transpose(pb[0:64, 0:32], PEs[:, kv, :], ident[0:32, 0:32]), reads=[PEs.b, ident.b], writes=[pb.b])
                        S.op("vector", lambda e: e.tensor_copy(PET[:, kv, :], pb[0:64, 0:32]), reads=[pb.b], writes=[PET.b])
                        pb = bank()
                        for j in range(32):
                            S.op("tensor", lambda e: e.matmul(pb[0:64, 0:1], lhsT=W1[0:64, kv, j, :], rhs=PET[:, kv, j:j + 1], start=(j == 0), stop=(j == 31)),
                                 reads=[W1.b, PET.b], writes=[pb.b], inc=(j == 31))
                        S.op("vector", lambda e: e.tensor_copy(CB[:, kv:kv + 1], pb[0:64, 0:1]), reads=[pb.b], writes=[CB.b])
                    for kv in range(2):
                        for g in range(2):
                            pb = bank()
                            for j in range(32):
                                S.op("tensor", lambda e: e.matmul(pb[0:64, 0:127], lhsT=W1[g * 64:(g + 1) * 64, kv, j, :],
                                                                  rhs=KCV[g * 64:(g + 1) * 64, kv, j:j + 16 * 126 + 1:16], start=(j == 0), stop=(j == 31)),
                                     reads=[W1.b, KCV.b], writes=[pb.b], inc=(j == 31))
                            gx, gt = GX[:, 0:127], GT[:, 0:127]
                            S.op("scalar", lambda e: e.activation(gx, pb[0:64, 0:127], AF.Identity, bias=CB[:, kv:kv + 1], scale=1.0), reads=[pb.b, CB.b], writes=[GX.b])
                            S.op("vector", lambda e: e.tensor_tensor(gt, gx, gx, ALU.mult), reads=[GX.b], writes=[GT.b])
                            S.op("vector", lambda e: e.tensor_scalar(gt, gt, 0.044715, 1.0, ALU.mult, ALU.add), reads=[GT.b], writes=[GT.b])
                            S.op("vector", lambda e: e.tensor_tensor(gt, gt, gx, ALU.mult), reads=[GT.b, GX.b], writes=[GT.b])
                            S.op("scalar", lambda e: e.activation(gt, gt, AF.Tanh, scale=0.7978845608028654), reads=[GT.b], writes=[GT.b])
                            S.op("vector", lambda e: e.tensor_scalar(gt, gt, 1.0, 0.5, ALU.add, ALU.mult), reads=[GT.b], writes=[GT.b])
                            S.op("vector", lambda e: e.tensor_tensor(GEL[:, 0:127], gt, gx, ALU.mult), reads=[GT.b, GX.b], writes=[GEL.b])
                            pb2 = bank()
                            if kv == 0:
                                S.op("tensor", lambda e: e.matmul(pb2[:, 0:127], lhsT=W2K[:], rhs=GEL[:, 0:127], start=True, stop=True), reads=[W2K.b, GEL.b], writes=[pb2.b])
                                S.op("vector", lambda e: e.tensor_copy(KCMP2[:, g, 0:127], pb2[:, 0:127]), reads=[pb2.b], writes=[KCMP2.b])
                            else:
                                S.op("tensor", lambda e: e.matmul(pb2[0:127, 0:64], lhsT=GEL[:, 0:127], rhs=W2V[:], start=True, stop=True), reads=[W2V.b, GEL.b], writes=[pb2.b])
                                S.op("vector", lambda e: e.tensor_copy(VCA[0:127, g, 0:64], pb2[0:127, 0:64]), reads=[pb2.b], writes=[VCA.b])
                    dump("kcmp", KCMP2[:], [KCMP2.b])
                    dump("vca", VCA[:], [VCA.b])
                    S.barrier()

                if stage == "cmp":
                    nst.close(); return
                with contextlib.ExitStack() as ns:
                    def nsb(name, shape, dtype):
                        return Tile(ns.enter_context(nc.sbuf_tensor(f"{name}_n{l}", shape, dtype)), name)
                    ET = [nsb(f"ET{i}", [128, 512], BF16) for i in range(3)]
                    ACC = [nsb(f"ACC{i}", [128, 4, 4, 64], F32) for i in range(2)]
                    IMP = nsb("IMP", [128, 4, 32], F32)
                    ADJ = nsb("ADJ", [128, 4, 32], F32)
                    M8 = nsb("M8", [128, 32], F32)
                    STS = [nsb(f"STS{i}", [128, 12], F32) for i in range(4)]
                    et_rr = [0]
                    st_rr = [0]

                    def stile(qap, kap, nk, c0, c1, bias, post, vap, ncol, ob, rd):
                        n = c1 - c0
                        pb = bank()
                        if bias is not None:
                            S.op("tensor", lambda e: e.matmul(pb[0:nk, 0:n], lhsT=bias[0], rhs=bias[1], start=True, stop=False),
                                 reads=[onehot2.b, SB2.b], writes=[pb.b], inc=False)
                        S.op("tensor", lambda e: e.matmul(pb[0:nk, 0:n], lhsT=kap, rhs=qap, start=(bias is None), stop=True), reads=rd, writes=[pb.b])
                        E = ET[et_rr[0] % 3]
                        et_rr[0] += 1
                        S.op("scalar", lambda e: e.activation(E[0:nk, c0:c1], pb[0:nk, 0:n], AF.Exp), reads=[pb.b], writes=[E.b])
                        for (mk, a, b_) in post:
                            S.op("vector", lambda e: e.tensor_tensor(E[0:nk, a:b_], E[0:nk, a:b_], mk, ALU.mult), reads=[E.b], writes=[E.b])
                        qts = list(range(c0 // 128, c1 // 128))

                        def pv(final=False):
                            for qt in qts:
                                S.op("tensor", lambda e: e.matmul(ob[:, qt * ncol:(qt + 1) * ncol], lhsT=E[0:nk, qt * 128:(qt + 1) * 128], rhs=vap, start=False,
                                                                  stop=(final and qt == qts[-1])),
                                     reads=[E.b] + rd, writes=[ob.b], inc=(qt == qts[-1]))
                        return pv

                    pend_fin = [None]

                    def flush_fin():
                        if pend_fin[0] is not None:
                            pend_fin[0]()
                            pend_fin[0] = None

                    def run_tiles(specs, fin):
                        pvs = None
                        for idx, sp in enumerate(specs):
                            nxt = stile(*sp)
                            if idx == 0:
                                flush_fin()
                            if pvs is not None:
                                pvs()
                            pvs = nxt
                        if pvs is not None:
                            pvs(final=True)
                        pend_fin[0] = fin

                    def finalize(ob, ncol, br, g, Q, r, acc):
                        hh = 4 * g + r
                        st = STS[st_rr[0] % 4]
                        st_rr[0] += 1
                        ob3 = ob[:, 0:4 * ncol].rearrange("p (q c) -> p q c", q=4)
                        S.op("vector", lambda e: e.tensor_scalar_max(st[:, 0:4], ob3[:, :, 64], 1e-30), reads=[ob.b], writes=[st.b])
                        S.op("vector", lambda e: e.reciprocal(st[:, 4:8], st[:, 0:4]), reads=[st.b], writes=[st.b])
                        S.op("vector", lambda e: e.tensor_tensor(st[:, 8:12], st[:, 4:8], GATES[:, 4 * Q:4 * Q + 4, hh * 3 + br], ALU.mult), reads=[st.b, GATES.b], writes=[st.b])
                        for qt in range(4):
                            if br == 0:
                                S.op("vector", lambda e: e.tensor_scalar(acc[:, qt, r, :], ob3[:, qt, 0:64], st[:, 8 + qt:9 + qt], None, ALU.mult), reads=[ob.b, st.b], writes=[acc.b])
                                if r == 0:
                                    S.op("vector", lambda e: e.tensor_scalar(IMP[:, qt, :], ob3[:, qt, 65:97], st[:, 4 + qt:5 + qt], None, ALU.mult), reads=[ob.b, st.b], writes=[IMP.b])
                                else:
                                    S.op("vector", lambda e: e.scalar_tensor_tensor(out=IMP[:, qt, :], in0=ob3[:, qt, 65:97], scalar=st[:, 4 + qt:5 + qt], in1=IMP[:, qt, :],
                                                                                    op0=ALU.mult, op1=ALU.add), reads=[ob.b, st.b, IMP.b], writes=[IMP.b])
                            else:
                                S.op("vector", lambda e: e.scalar_tensor_tensor(out=acc[:, qt, r, :], in0=ob3[:, qt, 0:64], scalar=st[:, 8 + qt:9 + qt], in1=acc[:, qt, r, :],
                                                                                op0=ALU.mult, op1=ALU.add), reads=[ob.b, st.b, acc.b], writes=[acc.b])

                    gq_i = 0
                    for g in range(2):
                        for Q in range(4):
                            acc = ACC[gq_i % 2]
                            gq_i += 1
                            qs = slice(Q * 512, (Q + 1) * 512)
                            for r in range(4):
                                hh = 4 * g + r
                                p, e_ = hh // 2, hh % 2
                                rows = slice(64 * e_, 64 * e_ + 64)
                                ob = acc_bank()
                                zero_bank(ob, 128, 4 * 97)
                                run_tiles([(QT2[rows, p, qs], KCMP2[rows, g, 0:127], 127, 0, 512, None, [(cmpmask[0:127, Q, :], 0, 512)],
                                            VCA[0:127, g, :], 97, ob, [QT2.b, KCMP2.b, VCA.b])],
                                          (lambda ob=ob, r=r: finalize(ob, 97, 0, g, Q, r, acc)))
                            flush_fin()
                            S.op("vector", lambda e: e.tensor_tensor(ADJ[:], IMP[:], keepm[:, 4 * Q:4 * Q + 4, :], ALU.mult), reads=[IMP.b, keepm.b], writes=[ADJ.b])
                            S.op("vector", lambda e: e.tensor_tensor(ADJ[:], ADJ[:], addm[:, 4 * Q:4 * Q + 4, :], ALU.add), reads=[ADJ.b, addm.b], writes=[ADJ.b])
                            pbs = bank()
                            for qt in range(4):
                                S.op("vector", lambda e: e.max(M8[:, qt * 8:(qt + 1) * 8], ADJ[:, qt, :]), reads=[ADJ.b], writes=[M8.b])
                                for off in (0, 64):
                                    S.op("vector", lambda e: e.tensor_scalar(SELS[:, qt, off:off + 32], ADJ[:, qt, :], M8[:, qt * 8 + 7:qt * 8 + 8], NEG, ALU.is_lt, ALU.mult),
                                         reads=[ADJ.b, M8.b], writes=[SELS.b])
                                S.op("tensor", lambda e: e.transpose(pbs[:, qt * 128:(qt + 1) * 128], SELS[:, qt, :], ident[:]), reads=[SELS.b, ident.b], writes=[pbs.b], inc=True)
                            S.op("vector", lambda e: e.tensor_copy(SB2[:, g, qs], pbs[:]), reads=[pbs.b], writes=[SB2.b])
                            for r in range(4):
                                hh = 4 * g + r
                                p, e_ = hh // 2, hh % 2
                                rows = slice(64 * e_, 64 * e_ + 64)
                                ob = acc_bank()
                                zero_bank(ob, 128, 4 * 65)
                                specs = []
                                for j in range(0, 4 * Q + 4):
                                    d = j - 4 * Q
                                    c0 = 128 * d if d > 0 else 0
                                    post = [(causal[:], 128 * d, 128 * d + 128)] if d >= 0 else []
                                    ks = slice(j * 128, (j + 1) * 128)
                                    specs.append((QT2[rows, p, Q * 512 + c0:(Q + 1) * 512], KS2[rows, g, ks], 128, c0, 512,
                                                  (onehot2[rows, ks], SB2[rows, g, Q * 512 + c0:(Q + 1) * 512]), post, VSA[:, j, g, :], 65, ob, [QT2.b, KS2.b, VSA.b]))
                                run_tiles(specs, (lambda ob=ob, r=r: finalize(ob, 65, 1, g, Q, r, acc)))
                                ob = acc_bank()
                                zero_bank(ob, 128, 4 * 65)
                                specs = []
                                for j in range(max(0, 4 * Q - 4), 4 * Q + 4):
                                    d = j - 4 * Q
                                    ks = slice(j * 128, (j + 1) * 128)
                                    if d >= 0:
                                        c0, c1 = 128 * d, 512
                                        post = [(causal[:], 128 * d, 128 * d + 128)]
                                    else:
                                        dd = d + 4
                                        c0, c1 = 0, 128 * (dd + 1)
                                        post = [(anti[:], 128 * dd, 128 * dd + 128)]
                                    specs.append((QT2[rows, p, Q * 512 + c0:Q * 512 + c1], KW2[rows, g, ks], 128, c0, c1, None, post, VWA[:, j, g, :], 65, ob, [QT2.b, KW2.b, VWA.b]))
                                run_tiles(specs, (lambda ob=ob, r=r: finalize(ob, 65, 2, g, Q, r, acc)))
                            flush_fin()
                            for b2 in range(2):
                                pbt = bank()
                                for qt in range(4):
                                    S.op("tensor", lambda e: e.transpose(pbt[:, qt * 128:(qt + 1) * 128], acc[:, qt, 2 * b2:2 * b2 + 2, :].rearrange("p r d -> p (r d)"), ident[:]),
                                         reads=[acc.b, ident.b], writes=[pbt.b], inc=(qt == 3))
                                evac(AT[:, 2 * g + b2, qs], pbt[:], [pbt.b], all_at[4 * Q:4 * Q + 4])
                    S.barrier()
                nst.close()

                with contextlib.ExitStack() as ps_:
                    PT = [Tile(ps_.enter_context(nc.sbuf_tensor(f"PT{i}_p{l}", [64, 4, 128], BF16)), f"PT{i}") for i in range(2)]
                    for t in range(NT):
                        pb = bank()
                        for g in range(4):
                            kind = 2 if t == 0 else 0
                            S.op("tensor", lambda e: e.matmul(pb[0:64, g * 128:(g + 1) * 128], lhsT=UT[:, t, g * 64:(g + 1) * 64], rhs=pbc[:, kind, g, :], start=True, stop=(t == 0)),
                                 reads=[UT.b, pbc.b], writes=[pb.b], inc=(t == 0 and g == 3))
                            if t > 0:
                                S.op("tensor", lambda e: e.matmul(pb[0:64, g * 128:(g + 1) * 128], lhsT=UT[:, t - 1, g * 64:(g + 1) * 64], rhs=pbc[:, 1, g, :], start=False, stop=True),
                                     reads=[UT.b, pbc.b], writes=[pb.b], inc=(g == 3))
                        pt = PT[t % 2]
                        S.op("vector", lambda e: e.tensor_copy(pt[:], pb[0:64, :].rearrange("p (g k) -> p g k", g=4)), reads=[pb.b], writes=[pt.b])
                        pb2 = bank()
                        for pr in range(2):
                            for gg in range(2):
                                g = pr * 2 + gg
                                S.op("tensor", lambda e: e.matmul(pb2[:, pr * 128:(pr + 1) * 128], lhsT=PWP[:, g, :], rhs=pt[:, g, :], start=(gg == 0), stop=(gg == 1)),
                                     reads=[PWP.b, pt.b], writes=[pb2.b], inc=(gg == 1))
                            S.op("scalar", lambda e: e.activation(AT[:, 4 + pr, t * 128:(t + 1) * 128], pb2[:, pr * 128:(pr + 1) * 128], AF.Copy, scale=PSC[:, pr:pr + 1]),
                                 reads=[pb2.b, PSC.b], writes=[AT.bs[t]])

                if stage == "pool":
                    S.barrier(); return
                with contextlib.ExitStack() as gs:
                    def gsb(name, shape, dtype):
                        return Tile(gs.enter_context(nc.sbuf_tensor(f"{name}_g{l}", shape, dtype)), name)
                    LA = gsb("LA", [128, 512], F32)
                    B16 = gsb("B16", [128, 512], F32)
                    EB = gsb("EB", [128, 512], F32)
                    ENB = gsb("ENB", [128, 512], F32)
                    QM = gsb("QM", [128, 4, 4, 128], BF16)
                    KT = gsb("KT", [128, 512], BF16)
                    KD = gsb("KD", [128, 512], F32)
                    KDT = gsb("KDT", [128, 4, 128], BF16)
                    SST = [gsb(f"SST{i}", [128, 256], F32) for i in range(2)]
                    SBF = gsb("SBF", [128, 9, 256], BF16)
                    AM = [gsb(f"AM{i}", [128, 4, 128], BF16) for i in range(2)]
                    OSB = [gsb(f"OSB{i}", [128, 256], F32) for i in range(2)]
                    SQ = gsb("SQ", [128, 256], F32)
                    OG = [gsb(f"OG{i}", [128, 256], F32) for i in range(2)]
                    RS = [gsb(f"RS{i}", [128, 8], F32) for i in range(2)]
                    S.op("vector", lambda e: e.memset(SST[0][:], 0.0), writes=[SST[0].b])
                    S.op("vector", lambda e: e.memset(SBF[:, 0, :], 0.0), writes=[SBF.b])
                    si = 0
                    for tg in range(4):
                        ts_ = slice(tg * 512, (tg + 1) * 512)
                        pb = bank()
                        S.op("tensor", lambda e: e.matmul(pb[:], lhsT=WA2[:], rhs=GA[0:16, ts_], start=True, stop=True), reads=[WA2.b, GA.b], writes=[pb.b])
                        S.op("scalar", lambda e: e.activation(LA[:], pb[:], AF.Sigmoid, bias=BA[:, 0:1], scale=1.0), reads=[pb.b, BA.b], writes=[LA.b])
                        S.op("scalar", lambda e: e.activation(LA[:], LA[:], AF.Ln), reads=[LA.b], writes=[LA.b])
                        S.op("vector", lambda e: e.tensor_tensor_scan(B16[:], scanmask[:], LA[:], 0.0, ALU.mult, ALU.add), reads=[LA.b, scanmask.b], writes=[B16.b])
                        S.op("scalar", lambda e: e.activation(EB[:], B16[:], AF.Exp, scale=1.0 / 16.0), reads=[B16.b], writes=[EB.b])
                        S.op("scalar", lambda e: e.activation(ENB[:], B16[:], AF.Exp, scale=-1.0 / 16.0), reads=[B16.b], writes=[ENB.b])
                        for h in range(4):
                            S.op("vector", lambda e: e.scalar_tensor_tensor(out=QM[:, :, h, :], in0=GQK[:, 0, ts_].rearrange("p (t k) -> p t k", t=4), scalar=hm[:, h:h + 1],
                                                                            in1=EB[:].rearrange("p (t k) -> p t k", t=4), op0=ALU.mult, op1=ALU.mult),
                                 reads=[GQK.b, hm.b, EB.b], writes=[QM.b])
                        S.op("vector", lambda e: e.tensor_tensor(KT[:], GQK[:, 1, ts_], ENB[:], ALU.mult), reads=[GQK.b, ENB.b], writes=[KT.b])
                        for c in range(8):
                            S.op("vector", lambda e: e.tensor_scalar(KD[:, c * 64:(c + 1) * 64], KT[:, c * 64:(c + 1) * 64], EB[:, c * 64 + 63:c * 64 + 64], None, ALU.mult),
                                 reads=[KT.b, EB.b], writes=[KD.b])
                        pb = bank()
                        for tt in range(4):
                            S.op("tensor", lambda e: e.transpose(pb[:, tt * 128:(tt + 1) * 128], KD[:, tt * 128:(tt + 1) * 128], ident[:]), reads=[KD.b, ident.b], writes=[pb.b], inc=(tt == 3))
                        S.op("vector", lambda e: e.tensor_copy(KDT[:], pb[:].rearrange("p (t k) -> p t k", t=4)), reads=[pb.b], writes=[KDT.b])
                        for c in range(8):
                            tt, hf = c // 2, c % 2
                            rows = slice(64 * hf, 64 * hf + 64)
                            pbs = bank()
                            S.op("tensor", lambda e: e.matmul(pbs[:, 0:256], lhsT=KDT[rows, tt, :], rhs=GV[rows, 4 * tg + tt, :], start=True, stop=True),
                                 reads=[KDT.b, GV.b], writes=[pbs.b])
                            so, sn = SST[si % 2], SST[(si + 1) % 2]
                            si += 1
                            S.op("vector", lambda e: e.scalar_tensor_tensor(out=sn[:], in0=so[:], scalar=EB[:, c * 64 + 63:c * 64 + 64], in1=pbs[:, 0:256], op0=ALU.mult, op1=ALU.add),
                                 reads=[so.b, EB.b, pbs.b], writes=[sn.b])
                            S.op("scalar", lambda e: e.activation(SBF[:, c + 1, :], sn[:], AF.Identity), reads=[sn.b], writes=[SBF.b])
                        for tt in range(4):
                            t = 4 * tg + tt
                            pa = bank()
                            S.op("tensor", lambda e: e.matmul(pa[:], lhsT=KT[:, tt * 128:(tt + 1) * 128], rhs=QM[:, tt, :, :].rearrange("p h k -> p (h k)"), start=True, stop=True),
                                 reads=[KT.b, QM.b], writes=[pa.b])
                            am = AM[tt % 2]
                            S.op("vector", lambda e: e.tensor_tensor(am[:], pa[:].rearrange("p (h k) -> p h k", h=4), gmask4[:], ALU.mult), reads=[pa.b, gmask4.b], writes=[am.b])
                            po = bank()
                            zero_bank(po, 128, 256)
                            for h in range(4):
                                for hf in range(2):
                                    S.op("tensor", lambda e: e.matmul(po[64 * hf:64 * hf + 64, 64 * h:64 * h + 64], lhsT=QM[:, tt, h, 64 * hf:64 * hf + 64], rhs=SBF[:, 2 * tt + hf, 64 * h:64 * h + 64],
                                                                      start=False, stop=False), reads=[QM.b, SBF.b], writes=[po.b], inc=False)
                                S.op("tensor", lambda e: e.matmul(po[:, 64 * h:64 * h + 64], lhsT=am[:, h, :], rhs=GV[:, t, 64 * h:64 * h + 64], start=False, stop=(h == 3)),
                                     reads=[am.b, GV.b], writes=[po.b], inc=(h == 3))
                            osb, og, rs = OSB[tt % 2], OG[tt % 2], RS[tt % 2]
                            S.op("vector", lambda e: e.tensor_copy(osb[:], po[:, 0:256]), reads=[po.b], writes=[osb.b])
                            S.op("vector", lambda e: e.tensor_tensor(SQ[:], osb[:], osb[:], ALU.mult), reads=[osb.b], writes=[SQ.b])
                            S.op("vector", lambda e: e.tensor_reduce(rs[:, 0:4], SQ[:].rearrange("p (h d) -> p h d", h=4), AX.X, ALU.add), reads=[SQ.b], writes=[rs.b])
                            S.op("scalar", lambda e: e.activation(rs[:, 4:8], rs[:, 0:4], AF.Sqrt, bias=epsc[:, 1:2], scale=1.0 / 64.0), reads=[rs.b, epsc.b], writes=[rs.b])
                            S.op("vector", lambda e: e.reciprocal(rs[:, 4:8], rs[:, 4:8]), reads=[rs.b], writes=[rs.b])
                            for h in range(4):
                                S.op("vector", lambda e: e.scalar_tensor_tensor(out=og[:, 64 * h:64 * h + 64], in0=osb[:, 64 * h:64 * h + 64], scalar=rs[:, 4 + h:5 + h],
                                                                                in1=NGSR[:, t, 64 * h:64 * h + 64], op0=ALU.mult, op1=ALU.mult), reads=[osb.b, rs.b, NGSR.b], writes=[og.b])
                            pt_ = bank()
                            for b2 in range(2):
                                S.op("tensor", lambda e: e.transpose(pt_[:, b2 * 128:(b2 + 1) * 128], og[:, b2 * 128:(b2 + 1) * 128], ident[:]), reads=[og.b, ident.b], writes=[pt_.b], inc=(b2 == 1))
                            evac(AT[:, 6:8, t * 128:(t + 1) * 128], pt_[:, 0:256].rearrange("p (c k) -> p c k", c=2), [pt_.b], [AT.bs[t]])
                        S.op("vector", lambda e: e.tensor_copy(SBF[:, 0, :], SBF[:, 8, :]), reads=[SBF.b], writes=[SBF.b])
                    S.barrier()

                if stage == "gla":
                    return
                dump("mixT", AT[:], all_at)
                nodma[0] = (stage == "lnnodma")
                if stage != "mm":
                    prefetch_res(0, False)
                pend = None
                for t in range(NT):
                    yb = [bank(), bank()]
                    for dh in range(2):
                        for k in range(8):
                            S.op("tensor", lambda e: e.matmul(yb[dh][:], lhsT=AT[:, k, t * 128:(t + 1) * 128], rhs=WO[dh][:, k, :], start=(k == 0), stop=(k == 7)),
                                 reads=[AT.bs[t], WO[dh].b], writes=[yb[dh].b], inc=(k == 7))
                    if stage == "mm":
                        continue
                    if pend is not None:
                        ln_post(*pend)
                    pend = (t, ln_tile(t, yb, 1.0 / ALPHA, False), False)
                    if t < NT - 1:
                        prefetch_res(t + 1, False)
                if pend is not None:
                    ln_post(*pend)
                nodma[0] = False
                S.barrier()

        for l in range(depth):
            ffn(l, 0, first=(l == 0), last=False)
            if mix:
                mixer(l)
            ffn(l, 1, first=False, last=(l == depth - 1))

        for c in d_xout:
            S._wait(S.E["sync"], (c, c.cnt))
        print("instructions:", S.n_ins)
    return nc


_NC_CACHE = {}


LAUNCH_DEPTH = 4


def kernel(**inputs):
    x = np.ascontiguousarray(inputs["x"], dtype=np.float32)
    B = x.shape[0]
    key = ("nc", LAUNCH_DEPTH)
    if key not in _NC_CACHE:
        _NC_CACHE[key] = build(LAUNCH_DEPTH)
    nc = _NC_CACHE[key]
    cst = consts()
    cur = x
    for l0 in range(0, DEPTH, LAUNCH_DEPTH):
        shared = {k: np.ascontiguousarray(v[l0:l0 + LAUNCH_DEPTH], dtype=np.float32) for k, v in inputs.items() if k != "x"}
        shared.update(cst)
        in_maps = []
        for b in range(B):
            m = dict(shared)
            m["x"] = np.ascontiguousarray(cur[b])
            in_maps.append(m)
        res = run_bass_kernel_spmd(nc, in_maps, core_ids=list(range(B)))
        cur = np.stack([r["y"] for r in res.results], axis=0)
    return cur
```
